# Optimizing a Trainium2 kernel written in Bass

```python
import math
import jax, jax.numpy as jnp
from jax import lax
import numpy as np

D_MODEL = 1024
BATCH = 8
SEQ = 2048
DEPTH = 4
DEC_BATCH = 2
DEC_SEQ = 16384
PAST_LEN = 128

EPS = 1e-6
N_MIXERS = 3
N_A = (DEPTH + 2) // 3
N_B = (DEPTH + 1) // 3
N_C = DEPTH // 3

D_FF = 2816

A_HIDDEN = 6 * D_MODEL
A_HALF = A_HIDDEN // 2
A_CHUNK = 128
A_GROUPS = 8
A_GROUP_DIM = A_HALF // A_GROUPS

B_PATTERNS = ((128, 1), (512, 4), (2048, 16))
B_N_GROUPS = len(B_PATTERNS)
B_HEADS_PER_GROUP = 6
B_HEAD_DIM = 64
B_N_HEADS = B_N_GROUPS * B_HEADS_PER_GROUP
B_WIDTH = B_N_HEADS * B_HEAD_DIM
REL_BUCKETS = 32
REL_MAX_DIST = 1024
NEG_INF = -1e30

CONV_WIDTH = 31
CONV_PAD = CONV_WIDTH // 2

kernel_name = "hybrid_gmlp_dilatedattn_conformer_encoder"


def rms_norm(x, g):
    xf = x.astype(jnp.float32)
    y = xf * lax.rsqrt(jnp.mean(xf * xf, axis=-1, keepdims=True) + EPS)
    return (y * g.astype(jnp.float32)).astype(x.dtype)


def swiglu(x, w_gate, w_up, w_down):
    return (jax.nn.silu(x @ w_gate) * (x @ w_up)) @ w_down


def chunked_gmlp(x, w_in, g_v, w_spatial, b_spatial, w_out):
    B, S, _ = x.shape
    h = jax.nn.gelu(x @ w_in)
    u, v = h[..., :A_HALF], h[..., A_HALF:]
    v = rms_norm(v, g_v)
    v = v.reshape(B, S // A_CHUNK, A_CHUNK, A_GROUPS, A_GROUP_DIM)
    v = jnp.einsum('gpq,bcqgd->bcpgd', w_spatial, v) + b_spatial.T[None, None, :, :, None]
    v = v.reshape(B, S, A_HALF)
    return (u * v) @ w_out


def t5_bucket(rel):
    half = REL_BUCKETS // 2
    max_exact = half // 2
    ret = jnp.where(rel > 0, half, 0)
    n = jnp.abs(rel)
    nf = jnp.maximum(n, 1).astype(jnp.float32)
    large = max_exact + (jnp.log(nf / max_exact) / math.log(REL_MAX_DIST / max_exact)
                         * (half - max_exact)).astype(jnp.int32)
    large = jnp.minimum(large, half - 1)
    return ret + jnp.where(n < max_exact, n, large)


def dilated_band_attention(q, k, v, window, dil, bias_table):
    B, S, H, Dh = q.shape
    w_half = window // (2 * dil)
    blk = w_half
    L = S // dil
    nb = -(-L // blk)
    Lp = nb * blk

    def to_residue(t):
        t = t.reshape(B, L, dil, H, Dh).transpose(0, 2, 1, 3, 4)
        return jnp.pad(t, ((0, 0), (0, 0), (0, Lp - L), (0, 0), (0, 0)))

    def band(t):
        tp = jnp.pad(to_residue(t), ((0, 0), (0, 0), (blk, blk), (0, 0), (0, 0)))
        tp = tp.reshape(B, dil, nb + 2, blk, H, Dh)
        return jnp.concatenate([tp[:, :, :-2], tp[:, :, 1:-1], tp[:, :, 2:]], axis=3)

    qr = to_residue(q).reshape(B, dil, nb, blk, H, Dh)
    kb, vb = band(k), band(v)
    scores = jnp.einsum('brcqhd,brckhd->brcqhk', qr, kb,
                        preferred_element_type=jnp.float32) * (1.0 / math.sqrt(Dh))

    q_idx = jnp.arange(blk, dtype=jnp.int32)
    k_idx = jnp.arange(3 * blk, dtype=jnp.int32)
    rel = k_idx[None, :] - blk - q_idx[:, None]
    bias = bias_table[t5_bucket(rel * dil)].astype(jnp.float32)
    bias = bias.transpose(0, 2, 1)
    kpos = (jnp.arange(nb, dtype=jnp.int32)[:, None] - 1) * blk + k_idx[None, :]
    valid = (jnp.abs(rel) <= w_half)[None] & ((kpos >= 0) & (kpos < L))[:, None, :]
    scores = jnp.where(valid[None, None, :, :, None, :], scores + bias[None, None, None], NEG_INF)

    lse = jax.nn.logsumexp(scores, axis=-1)
    probs = jnp.exp(scores - lse[..., None]).astype(v.dtype)
    out = jnp.einsum('brcqhk,brckhd->brcqhd', probs, vb)

    out = out.reshape(B, dil, Lp, H, Dh)[:, :, :L].transpose(0, 2, 1, 3, 4).reshape(B, S, H, Dh)
    lse = lse.reshape(B, dil, Lp, H)[:, :, :L].transpose(0, 2, 1, 3).reshape(B, S, H)
    return out, lse


def dilated_attention_mixer(x, w_qkv, w_out, rel_bias):
    B, S, _ = x.shape
    qkv = (x @ w_qkv).reshape(B, S, 3, B_N_GROUPS, B_HEADS_PER_GROUP, B_HEAD_DIM)
    outs, lses = [], []
    for g, (window, dil) in enumerate(B_PATTERNS):
        table = rel_bias[:, g * B_HEADS_PER_GROUP:(g + 1) * B_HEADS_PER_GROUP]
        o, l = dilated_band_attention(qkv[:, :, 0, g], qkv[:, :, 1, g], qkv[:, :, 2, g], window, dil, table)
        outs.append(o)
        lses.append(l)
    out = jnp.stack(outs, axis=2)
    alpha = jax.nn.softmax(jnp.stack(lses, axis=2), axis=2)
    out = out * alpha[..., None].astype(out.dtype)
    return out.reshape(B, S, B_WIDTH) @ w_out


def conformer_conv(x, w_pw1, b_pw1, w_dw, b_dw, g_norm, w_pw2, b_pw2):
    h = x @ w_pw1 + b_pw1
    h = h[..., :D_MODEL] * jax.nn.sigmoid(h[..., D_MODEL:])
    h = lax.conv_general_dilated(h, w_dw[:, None, :].astype(h.dtype), window_strides=(1,),
                                 padding=[(CONV_PAD, CONV_PAD)],
                                 dimension_numbers=('NWC', 'WIO', 'NWC'),
                                 feature_group_count=D_MODEL) + b_dw
    h = jax.nn.silu(rms_norm(h, g_norm))
    return h @ w_pw2 + b_pw2


def _trunk(x, p):
    for i in range(DEPTH):
        h = rms_norm(x, p['norm_ffn1'][i])
        x = x + 0.5 * swiglu(h, p['ffn1_w_gate'][i], p['ffn1_w_up'][i], p['ffn1_w_down'][i])
        h = rms_norm(x, p['norm_mix'][i])
        kind, j = i % N_MIXERS, i // N_MIXERS
        if kind == 0:
            m = chunked_gmlp(h, p['a_w_in'][j], p['a_g_v'][j], p['a_w_spatial'][j],
                             p['a_b_spatial'][j], p['a_w_out'][j])
        elif kind == 1:
            m = dilated_attention_mixer(h, p['b_w_qkv'][j], p['b_w_out'][j], p['rel_bias'])
        else:
            m = conformer_conv(h, p['c_w_pw1'][j], p['c_b_pw1'][j], p['c_w_dw'][j], p['c_b_dw'][j],
                               p['c_g_norm'][j], p['c_w_pw2'][j], p['c_b_pw2'][j])
        x = x + m
        h = rms_norm(x, p['norm_ffn2'][i])
        x = x + 0.5 * swiglu(h, p['ffn2_w_gate'][i], p['ffn2_w_up'][i], p['ffn2_w_down'][i])
    return rms_norm(x, p['norm_final'])


def _normal(key, shape, scale):
    return jax.random.normal(key, shape, jnp.float32) * scale


def setup_inputs(seed: int = 0) -> dict:
    key = jax.random.key(seed)
    ks = jax.random.split(key, 32)
    gain = lambda k, shape: 1.0 + _normal(k, shape, 0.02)
    return {
        'x_prompt': _normal(ks[0], (BATCH, SEQ, D_MODEL), 1.0),
        'x_sample': _normal(ks[1], (DEC_BATCH, DEC_SEQ, D_MODEL), 1.0),
        'norm_ffn1': gain(ks[2], (DEPTH, D_MODEL)),
        'ffn1_w_gate': _normal(ks[3], (DEPTH, D_MODEL, D_FF), D_MODEL ** -0.5),
        'ffn1_w_up': _normal(ks[4], (DEPTH, D_MODEL, D_FF), D_MODEL ** -0.5),
        'ffn1_w_down': _normal(ks[5], (DEPTH, D_FF, D_MODEL), D_FF ** -0.5),
        'norm_mix': gain(ks[6], (DEPTH, D_MODEL)),
        'norm_ffn2': gain(ks[7], (DEPTH, D_MODEL)),
        'ffn2_w_gate': _normal(ks[8], (DEPTH, D_MODEL, D_FF), D_MODEL ** -0.5),
        'ffn2_w_up': _normal(ks[9], (DEPTH, D_MODEL, D_FF), D_MODEL ** -0.5),
        'ffn2_w_down': _normal(ks[10], (DEPTH, D_FF, D_MODEL), D_FF ** -0.5),
        'a_w_in': _normal(ks[11], (N_A, D_MODEL, A_HIDDEN), D_MODEL ** -0.5),
        'a_g_v': gain(ks[12], (N_A, A_HALF)),
        'a_w_spatial': _normal(ks[13], (N_A, A_GROUPS, A_CHUNK, A_CHUNK), A_CHUNK ** -0.5),
        'a_b_spatial': 1.0 + _normal(ks[14], (N_A, A_GROUPS, A_CHUNK), 0.02),
        'a_w_out': _normal(ks[15], (N_A, A_HALF, D_MODEL), A_HALF ** -0.5),
        'b_w_qkv': _normal(ks[16], (N_B, D_MODEL, 3 * B_WIDTH), D_MODEL ** -0.5),
        'b_w_out': _normal(ks[17], (N_B, B_WIDTH, D_MODEL), B_WIDTH ** -0.5),
        'rel_bias': _normal(ks[18], (REL_BUCKETS, B_N_HEADS), 0.5),
        'c_w_pw1': _normal(ks[19], (N_C, D_MODEL, 2 * D_MODEL), D_MODEL ** -0.5),
        'c_b_pw1': _normal(ks[20], (N_C, 2 * D_MODEL), 0.02),
        'c_w_dw': _normal(ks[21], (N_C, CONV_WIDTH, D_MODEL), CONV_WIDTH ** -0.5),
        'c_b_dw': _normal(ks[22], (N_C, D_MODEL), 0.02),
        'c_g_norm': gain(ks[23], (N_C, D_MODEL)),
        'c_w_pw2': _normal(ks[24], (N_C, D_MODEL, D_MODEL), D_MODEL ** -0.5),
        'c_b_pw2': _normal(ks[25], (N_C, D_MODEL), 0.02),
        'norm_final': gain(ks[26], (D_MODEL,)),
    }


def reference(x_prompt, x_sample, norm_ffn1, ffn1_w_gate, ffn1_w_up, ffn1_w_down, norm_mix,
              norm_ffn2, ffn2_w_gate, ffn2_w_up, ffn2_w_down, a_w_in, a_g_v, a_w_spatial,
              a_b_spatial, a_w_out, b_w_qkv, b_w_out, rel_bias, c_w_pw1, c_b_pw1, c_w_dw,
              c_b_dw, c_g_norm, c_w_pw2, c_b_pw2, norm_final):
    p = dict(norm_ffn1=norm_ffn1, ffn1_w_gate=ffn1_w_gate, ffn1_w_up=ffn1_w_up,
             ffn1_w_down=ffn1_w_down, norm_mix=norm_mix, norm_ffn2=norm_ffn2,
             ffn2_w_gate=ffn2_w_gate, ffn2_w_up=ffn2_w_up, ffn2_w_down=ffn2_w_down,
             a_w_in=a_w_in, a_g_v=a_g_v, a_w_spatial=a_w_spatial, a_b_spatial=a_b_spatial,
             a_w_out=a_w_out, b_w_qkv=b_w_qkv, b_w_out=b_w_out, rel_bias=rel_bias,
             c_w_pw1=c_w_pw1, c_b_pw1=c_b_pw1, c_w_dw=c_w_dw, c_b_dw=c_b_dw,
             c_g_norm=c_g_norm, c_w_pw2=c_w_pw2, c_b_pw2=c_b_pw2, norm_final=norm_final)
    y_prompt = _trunk(x_prompt, p)
    y_sample = _trunk(x_sample, p)
    return (y_prompt, y_sample)
```

```python
import math
from contextlib import ExitStack

import numpy as np
import concourse.bass as bass
import concourse.mybir as mybir
from concourse.bass_utils import run_bass_kernel_spmd

F32 = mybir.dt.float32
BF16 = mybir.dt.bfloat16
AF = mybir.ActivationFunctionType
ALU = mybir.AluOpType

D = 1024
DFF = 2816
KC = 8
NFC = 22
AH = 3072
NP_ = 2048
HALO = 1152
OWN = 4096
NS = OWN + 2 * HALO
NL = NP_ + NS
QH = 16
EPS = 1e-6
NCORES = 8
TT = 512
NSLOT = 5
SLOT_ELEMS = 4096
DILS = (1, 4, 16)
VW = 1152 + 128
B0_SKEW = False

ENGS = ('pe', 'act', 'dve', 'pool', 'sp')
COMPUTE = ('pe', 'act', 'dve', 'pool')
BLOCKNAME = {'pe': 'tensor', 'act': 'scalar', 'dve': 'vector', 'pool': 'gpsimd', 'sp': 'sync'}
KDMA = 8


def _vec_layout():
    off = {}
    n = 0
    for l in range(4):
        off[('nf1', l)] = n; n += 8
        off[('nmx', l)] = n; n += 8
        off[('nf2', l)] = n; n += 8
    off['nfin'] = n; n += 8
    for j in range(2):
        off[('agv', j)] = n; n += 24
    off['cb1'] = n; n += 16
    off['cdw'] = n; n += 31 * 8
    off['cbdw'] = n; n += 8
    off['cgn'] = n; n += 8
    off['cb2'] = n; n += 8
    return off, n


VOFF, NV = _vec_layout()


class Op:
    __slots__ = ('eng', 'fn', 'deps', 'signal', 'sig', 'cover', 'dma', 'sem', 'semval')

    def __init__(self, eng, fn, dma):
        self.eng = eng
        self.fn = fn
        self.deps = []
        self.signal = False
        self.sig = None
        self.cover = None
        self.dma = dma
        self.sem = None
        self.semval = 0


class Sched:
    def __init__(self, nc, es):
        self.nc = nc
        self.csem = {e: es.enter_context(nc.semaphore("c_" + e)) for e in COMPUTE}
        self.dsem = {q: [es.enter_context(nc.semaphore("d_%s%d" % (q, i))) for i in range(KDMA)]
                     for q in ('sp', 'pool')}
        self.dcount = {'sp': 0, 'pool': 0}
        self.sigcount = {e: 0 for e in COMPUTE}
        self.pending = {e: [] for e in ENGS}
        self.lastw = {}
        self.readers = {}
        self.seen = {e: {} for e in ENGS}
        self.prefix = []
        self.nops = 0

    def op(self, eng, fn, reads=(), writes=(), dma=False):
        o = Op(eng, fn, dma)
        self.nops += 1
        cand = []
        for k in reads:
            w = self.lastw.get(k)
            if w is not None:
                cand.append((w, 0))
        for k in writes:
            w = self.lastw.get(k)
            if w is not None:
                cand.append((w, 1))
            rd = self.readers.get(k)
            if rd:
                for r in rd.values():
                    if isinstance(r, list):
                        for r2 in r:
                            cand.append((r2, 2))
                    else:
                        cand.append((r, 2))
        seen_ids = set()
        for p, kind in cand:
            if p is o or id(p) in seen_ids:
                continue
            if p.eng == eng and (not p.dma) and (not dma):
                if eng == 'pe':
                    continue
                if kind != 0:
                    continue
            seen_ids.add(id(p))
            o.deps.append(p)
            if not p.dma:
                p.signal = True
        for k in reads:
            rd = self.readers.setdefault(k, {})
            if dma:
                rd.setdefault('dma_' + eng, []).append(o)
            else:
                rd[eng] = o
        for k in writes:
            self.lastw[k] = o
            self.readers[k] = {}
        if dma:
            n = self.dcount[eng]
            self.dcount[eng] = n + 1
            o.sem = self.dsem[eng][n % KDMA]
            o.semval = 16 * (n // KDMA + 1)
        self.pending[eng].append(o)
        return o

    def _all_final(self):
        fin = []
        for e in COMPUTE:
            if self.sigcount[e] > 0:
                fin.append((('c', e), self.csem[e], self.sigcount[e]))
        for q in ('sp', 'pool'):
            n = self.dcount[q]
            for i in range(KDMA):
                cnt = (n - i + KDMA - 1) // KDMA if n > i else 0
                if cnt > 0:
                    fin.append((('d', q, i), self.dsem[q][i], 16 * cnt))
        return fin

    def flush(self, final=False, skip_pool_prefix=False):
        nc = self.nc
        for e in COMPUTE:
            ops = [o for o in self.pending[e] if not o.dma]
            if ops:
                ops[-1].signal = True
            cnt = self.sigcount[e]
            for o in ops:
                if o.signal:
                    cnt += 1
                    o.sig = cnt
            self.sigcount[e] = cnt
            nxt = None
            for o in reversed(ops):
                if o.signal:
                    nxt = o.sig
                o.cover = nxt
        prefix = self.prefix
        fin = self._all_final() if final else None
        dkey = {}
        for q in ('sp', 'pool'):
            for i in range(KDMA):
                dkey[id(self.dsem[q][i])] = ('d', q, i)
        with nc.Block() as block:
            for e in ENGS:
                ops = self.pending[e]

                def body(eng, e=e, ops=ops):
                    seen = self.seen[e]
                    for key, sem, val in prefix:
                        if seen.get(key, 0) < val:
                            eng.wait_ge(sem, val)
                            seen[key] = val
                    for o in ops:
                        waits = {}
                        for p in o.deps:
                            if p.dma:
                                key, s, v = dkey[id(p.sem)], p.sem, p.semval
                            else:
                                key, s, v = ('c', p.eng), self.csem[p.eng], p.cover
                            if key not in waits or waits[key][1] < v:
                                waits[key] = (s, v)
                        if o.dma and o.semval > 16:
                            key = dkey[id(o.sem)]
                            v = o.semval - 16
                            if key not in waits or waits[key][1] < v:
                                waits[key] = (o.sem, v)
                        for key, (s, v) in waits.items():
                            if seen.get(key, 0) < v:
                                eng.wait_ge(s, v)
                                seen[key] = v
                        inst = o.fn(eng)
                        if o.dma:
                            inst.then_inc(o.sem, 16)
                        elif o.signal:
                            inst.then_inc(self.csem[e], 1)
                    if fin is not None and e == 'sp':
                        for key, sem, val in fin:
                            if seen.get(key, 0) < val:
                                eng.wait_ge(sem, val)
                                seen[key] = val

                getattr(block, BLOCKNAME[e])(body)
        self.pending = {e: [] for e in ENGS}
        self.prefix = self._all_final()
        if skip_pool_prefix:
            self.prefix = [p for p in self.prefix if not (p[0][0] == 'd' and p[0][1] == 'pool')]


def dap(handle, offset, dims):
    return bass.AP(handle, offset, [[int(s), int(n)] for s, n in dims])


class Builder:
    def __init__(self, stop_after=None, dbg=False):
        self.stop_after = stop_after
        self.dbg = dbg
        self.nc = bass.Bass("TRN2", target_bir_lowering=False)
        self.es = ExitStack()
        self.wslot_n = 0
        self.mm_n = 0
        self.out_n = 0
        self.tmp_n = 0
        self.norm_pre = None
        self.convq = {}
        self.conv_rest = []

    def declare(self):
        nc = self.nc

        def inp(name, shape):
            return nc.dram_tensor(name, list(shape), F32, kind="ExternalInput")

        self.xl = inp("xl", [NL, D])
        self.w = {}
        for nm, shp in (
            ("ffn1_w_gate", [4, D, DFF]), ("ffn1_w_up", [4, D, DFF]), ("ffn1_w_down", [4, DFF, D]),
            ("ffn2_w_gate", [4, D, DFF]), ("ffn2_w_up", [4, D, DFF]), ("ffn2_w_down", [4, DFF, D]),
            ("a_w_in", [2, D, 2 * AH]), ("a_w_out", [2, AH, D]),
            ("a_w_spatial", [2, 8, 128, 128]), ("a_b_spatial", [2, 8, 128]), ("a_g_v", [2, AH]),
            ("b_w_qkv", [1, D, 3456]), ("b_w_out", [1, 1152, D]),
            ("c_w_pw1", [1, D, 2 * D]), ("c_w_pw2", [1, D, D]),
        ):
            self.w[nm] = inp(nm, shp)
        self.w_agv = self.w['a_g_v']
        self.vecs_d = inp("vecs", [128, NV])
        self.validcol_d = inp("validcol", [128, NL // 128])
        self.validrow_d = inp("validrow", [1, NL])
        self.ident_d = inp("ident", [128, 128])
        self.braw0_d = inp("braw0", [128, 18 * 256])
        self.braw1_d = inp("braw1", [128, 18 * 128])
        self.m4_d = inp("m4", [4, 4])
        self.y = nc.dram_tensor("y", [NP_ + OWN, D], F32, kind="ExternalOutput")

        def scratch(name, shape, dt):
            kind = "ExternalOutput" if (self.dbg and name in self.dbg) else "Internal"
            return nc.dram_tensor(name, list(shape), dt, kind=kind)

        self.XS1 = scratch("XS1", [D, NL], F32)
        self.XS2 = scratch("XS2", [D, NL], F32)
        self.GT = scratch("GT", [D, NL], BF16)
        self.QT = scratch("QT", [1152, NL], BF16)
        self.KT = scratch("KT", [1152, NL], BF16)
        self.VV = scratch("VV", [NL, VW], BF16)
        self.OT = scratch("OT", [1152, NL], BF16)
        self.BL = scratch("BL", [2, 4, AH], BF16)
        self.BR = scratch("BR", [2, 4, 4096], BF16)
        self.streams = {}
        for i in range(8):
            self.streams['f1_%d' % i] = (11, 4096)
            self.streams['f2_%d' % i] = (8, 2816)
        for j in range(2):
            self.streams['ain_%d' % j] = (12, 4096)
            self.streams['aout_%d' % j] = (8, 3072)
        self.streams['qkv'] = (9, 3072)
        self.streams['wo'] = (8, 1152)
        self.streams['pw1'] = (4, 4096)
        self.streams['pw2'] = (8, 1024)
        self.streams['dwd'] = (8, 3968)
        self.wbf = {}
        for nm, (npc, el) in self.streams.items():
            self.wbf[nm] = nc.dram_tensor("wbf_" + nm, [npc, 128, el], BF16, kind="Internal")

    def conv_dma(self, dst_h, dst_off, dst_dims, src_h, src_off, src_dims, key=None):
        out_ap = dap(dst_h, dst_off, dst_dims)
        in_ap = dap(src_h, src_off, src_dims)
        self.convq.setdefault(key[0], []).append((out_ap, in_ap, key))

    def emit_conv(self, streams=None, n=None):
        S = self.S
        todo = []
        if streams is not None:
            for st in streams:
                todo += self.convq.pop(st, [])
        else:
            while n > 0 and self.conv_rest:
                st = self.conv_rest[0]
                lst = self.convq.get(st, [])
                while lst and n > 0:
                    todo.append(lst.pop(0))
                    n -= 1
                if not lst:
                    self.conv_rest.pop(0)
                    self.convq.pop(st, None)
        for out_ap, in_ap, key in todo:
            S.op('pool', lambda g, out_ap=out_ap, in_ap=in_ap: g.dma_start(out=out_ap, in_=in_ap), writes=[key], dma=True)

    def phase0(self):
        for i in range(8):
            l = i // 2
            pre = "ffn1" if i % 2 == 0 else "ffn2"
            wg, wu, wd = self.w[pre + "_w_gate"], self.w[pre + "_w_up"], self.w[pre + "_w_down"]
            dst = self.wbf['f1_%d' % i]
            for fb in range(11):
                for gu, src in enumerate((wg, wu)):
                    self.conv_dma(dst, fb * 128 * 4096 + gu * 2048, [(4096, 128), (256, 8), (1, 256)],
                                  src, l * D * DFF + fb * 256, [(DFF, 128), (128 * DFF, 8), (1, 256)], key=('f1_%d' % i, fb, gu))
            dst = self.wbf['f2_%d' % i]
            for dc in range(8):
                self.conv_dma(dst, dc * 128 * 2816, [(2816, 128), (128, 22), (1, 128)],
                              wd, l * DFF * D + dc * 128, [(D, 128), (128 * D, 22), (1, 128)], key=('f2_%d' % i, dc, 0))
        for j in range(2):
            dst = self.wbf['ain_%d' % j]
            for pc in range(12):
                self.conv_dma(dst, pc * 128 * 4096, [(4096, 128), (512, 8), (1, 512)],
                              self.w['a_w_in'], j * D * 2 * AH + pc * 512,
                              [(2 * AH, 128), (128 * 2 * AH, 8), (1, 512)], key=('ain_%d' % j, pc, 0))
            dst = self.wbf['aout_%d' % j]
            for dc in range(8):
                self.conv_dma(dst, dc * 128 * 3072, [(3072, 128), (128, 24), (1, 128)],
                              self.w['a_w_out'], j * AH * D + dc * 128, [(D, 128), (128 * D, 24), (1, 128)], key=('aout_%d' % j, dc, 0))
        dst = self.wbf['qkv']
        for pc in range(9):
            self.conv_dma(dst, pc * 128 * 3072, [(3072, 128), (384, 8), (1, 384)],
                          self.w['b_w_qkv'], pc * 384, [(3456, 128), (128 * 3456, 8), (1, 384)], key=('qkv', pc, 0))
        dst = self.wbf['wo']
        for dc in range(8):
            self.conv_dma(dst, dc * 128 * 1152, [(1152, 128), (128, 9), (1, 128)],
                          self.w['b_w_out'], dc * 128, [(D, 128), (128 * D, 9), (1, 128)], key=('wo', dc, 0))
        dst = self.wbf['pw1']
        for fb in range(4):
            for ag in range(2):
                self.conv_dma(dst, fb * 128 * 4096 + ag * 2048, [(4096, 128), (256, 8), (1, 256)],
                              self.w['c_w_pw1'], ag * D + fb * 256, [(2 * D, 128), (128 * 2 * D, 8), (1, 256)], key=('pw1', fb, ag))
        dst = self.wbf['pw2']
        for dc in range(8):
            self.conv_dma(dst, dc * 128 * 1024, [(1024, 128), (128, 8), (1, 128)],
                          self.w['c_w_pw2'], dc * 128, [(D, 128), (128 * D, 8), (1, 128)], key=('pw2', dc, 0))

    def load_consts(self):
        S, nc = self.S, self.nc
        es = self.es
        self.vecs = es.enter_context(nc.sbuf_tensor("vecs_sb", [128, NV], F32))
        self.validcol = es.enter_context(nc.sbuf_tensor("validcol_sb", [128, NL // 128], F32))
        self.ident = es.enter_context(nc.sbuf_tensor("ident_sb", [128, 128], F32))
        self.ones_mean = es.enter_context(nc.sbuf_tensor("ones_mean", [128, 128], BF16))
        self.ones_bf = es.enter_context(nc.sbuf_tensor("ones_bf", [128, 128], BF16))
        self.wst32 = es.enter_context(nc.sbuf_tensor("wst32", [128, 2, 8, 128], F32))
        self.ps = es.enter_context(nc.psum_tensor("ps", [128, 8, 512], F32))
        S.op('sp', lambda q: q.dma_start(out=self.vecs[:], in_=self.vecs_d.ap()), writes=['vecs'], dma=True)
        S.op('sp', lambda q: q.dma_start(out=self.validcol[:], in_=self.validcol_d.ap()), writes=['validcol'],
             dma=True)
        S.op('sp', lambda q: q.dma_start(out=self.ident[:], in_=self.ident_d.ap()), writes=['ident'], dma=True)
        S.op('dve', lambda v: v.memset(self.ones_mean[:], 1.0 / 1024.0), writes=['ones_mean'])
        S.op('dve', lambda v: v.memset(self.ones_bf[:], 1.0), writes=['ones_bf'])

    def load_wst(self, wsin):
        S = self.S
        ps = self.ps
        for j in range(2):
            for g in range(8):
                k = j * 8 + g
                src = dap(self.w['a_w_spatial'], k * 128 * 128, [(128, 128), (1, 128)])
                buf = wsin[:, k % 2, :]
                S.op('sp', lambda q, buf=buf, src=src: q.dma_start(out=buf, in_=src),
                     writes=[('wsin', k % 2)], dma=True)
                bank = 6 + (k % 2)
                S.op('pe', lambda t, buf=buf, bank=bank: t.transpose(ps[:, bank, 0:128], buf, self.ident[:]),
                     reads=[('wsin', k % 2), 'ident'], writes=[('ps', bank)])
                S.op('act', lambda a, bank=bank, j=j, g=g: a.activation(out=self.wst32[:, j, g, :],
                                                                           in_=ps[:, bank, 0:128], func=AF.Copy),
                     reads=[('ps', bank)], writes=['wst32'])

    def build_bias_rows(self):
        S, nc = self.S, self.nc
        with nc.sbuf_tensor("bb32", [4, AH], F32) as t32, nc.sbuf_tensor("bbhi", [4, AH], BF16) as thi, \
                nc.sbuf_tensor("bblo32", [4, AH], F32) as tlo32, nc.sbuf_tensor("bblo", [4, AH], BF16) as tlo, \
                nc.sbuf_tensor("bbfin", [4, AH], BF16) as tfin, nc.sbuf_tensor("m4sb", [4, 4], F32) as m4:
            S.op('sp', lambda q: q.dma_start(out=m4[:], in_=self.m4_d.ap()), writes=['m4'], dma=True)
            for j in range(2):
                for kind in range(2):
                    n = AH if kind == 0 else 1024
                    if kind == 0:
                        src = dap(self.w_agv, j * AH, [(0, 4), (1, AH)])
                    else:
                        src = dap(self.w['a_b_spatial'], j * 1024, [(0, 4), (1, 1024)])
                    S.op('sp', lambda q, src=src, n=n: q.dma_start(out=t32[:, 0:n], in_=src), writes=['bb32'], dma=True)
                    if kind == 0:
                        S.op('dve', lambda v, n=n: v.reciprocal(out=t32[:, 0:n], in_=t32[:, 0:n]), reads=['bb32'], writes=['bb32'])
                    S.op('dve', lambda v, n=n: v.tensor_copy(out=thi[:, 0:n], in_=t32[:, 0:n]), reads=['bb32'], writes=['bbhi'])
                    S.op('dve', lambda v, n=n: v.tensor_tensor(out=tlo32[:, 0:n], in0=t32[:, 0:n], in1=thi[:, 0:n], op=ALU.subtract),
                         reads=['bb32', 'bbhi'], writes=['bblo32'])
                    S.op('dve', lambda v, n=n: v.tensor_copy(out=tlo[:, 0:n], in_=tlo32[:, 0:n]), reads=['bblo32'], writes=['bblo'])
                    mc = 0 if kind == 0 else 2
                    S.op('dve', lambda v, n=n, mc=mc: v.tensor_scalar(out=tfin[:, 0:n], in0=thi[:, 0:n], scalar1=m4[:, mc:mc + 1],
                                                                      scalar2=None, op0=ALU.mult),
                         reads=['bbhi', 'm4'], writes=['bbfin'])
                    S.op('dve', lambda v, n=n, mc=mc: v.scalar_tensor_tensor(out=tfin[:, 0:n], in0=tlo[:, 0:n], scalar=m4[:, mc + 1:mc + 2],
                                                                             in1=tfin[:, 0:n], op0=ALU.mult, op1=ALU.add),
                         reads=['bblo', 'bbfin', 'm4'], writes=['bbfin'])
                    if kind == 0:
                        dst = dap(self.BL, j * 4 * AH, [(AH, 4), (1, AH)])
                        S.op('sp', lambda q, dst=dst: q.dma_start(out=dst, in_=tfin[:, 0:AH]), reads=['bbfin'], writes=[('BL', j)], dma=True)
                    else:
                        for sb in range(4):
                            dst = dap(self.BR, j * 4 * 4096 + sb * 128, [(4096, 4), (512, 8), (1, 128)])
                            S.op('sp', lambda q, dst=dst: q.dma_start(out=dst, in_=tfin[:, 0:1024].rearrange("p (a b) -> p a b", b=128)),
                                 reads=['bbfin'], writes=[('BR', j, sb)], dma=True)

    def wload(self, stream, piece):
        S = self.S
        npc, el = self.streams[stream]
        slot = self.wslot_n % NSLOT
        self.wslot_n += 1
        src = dap(self.wbf[stream], piece * 128 * el, [(el, 128), (1, el)])
        dst = self.wring[:, slot, 0:el]
        rk = [(stream, piece, 0)]
        if stream.startswith('f1_') or stream == 'pw1':
            rk.append((stream, piece, 1))
        S.op('sp', lambda q: q.dma_start(out=dst, in_=src), reads=rk, writes=[('w', slot)], dma=True)
        return slot

    def wview(self, slot, dims):
        base = self.wring[:, slot, :]
        t, off, pst = base.tensor, base.offset, base.ap[0][0]
        strides = []
        s = 1
        for d in reversed(dims):
            strides.append(s)
            s *= d
        strides = list(reversed(strides))

        def view(idx, last):
            o = off
            for i, st in zip(idx, strides[:-1]):
                o = o + i * st
            o = o + last[0]
            return bass.AP(t, o, [[pst, 128], [1, last[1]]])

        return view

    def mmbank(self):
        b = self.mm_n % 4
        self.mm_n += 1
        return b

    def outbank(self):
        b = 4 + self.out_n % 2
        self.out_n += 1
        return b

    def tmpslot(self):
        b = self.tmp_n % 2
        self.tmp_n += 1
        return b

    def rmsnorm(self, T, src, src_keys, gcol, dst_fn, dst_keys, dst_eng='dve', post=None):
        S, ps = self.S, self.ps
        hT = self.hT
        pre = (src_keys[0] == ('xT', 0)) and (self.norm_pre == T)
        self.norm_pre = None
        if not pre:
            for c in range(KC):
                if c % 2 == 0:
                    S.op('act', lambda a, c=c: a.activation(out=hT[:, c, :T], in_=src(c), func=AF.Square),
                         reads=[src_keys[c]], writes=[('hT', c)])
                else:
                    S.op('dve', lambda v, c=c: v.tensor_tensor(out=hT[:, c, :T], in0=src(c), in1=src(c), op=ALU.mult),
                         reads=[src_keys[c]], writes=[('hT', c)])
            for c in range(KC):
                S.op('pe', lambda t, c=c: t.matmul(ps[:, 6, :T], lhsT=self.ones_mean[:], rhs=hT[:, c, :T],
                                                     start=(c == 0), stop=(c == KC - 1)),
                     reads=[('hT', c), 'ones_mean'], writes=[('ps', 6)])
        S.op('act', lambda a: a.activation(out=self.rt[:, 0, :T], in_=ps[:, 6, :T], func=AF.Ln, bias=self.epsc[:, 0:1]),
             reads=[('ps', 6), 'epsc'], writes=['rt0'])
        S.op('act', lambda a: a.activation(out=self.rt[:, 1, :T], in_=self.rt[:, 0, :T], func=AF.Exp, scale=-0.5),
             reads=['rt0'], writes=['rstd'])
        for c in range(KC):
            S.op('dve', lambda v, c=c: v.scalar_tensor_tensor(out=dst_fn(c), in0=src(c), scalar=self.vecs[:, gcol + c:gcol + c + 1],
                                                              in1=self.rt[:, 1, :T], op0=ALU.mult, op1=ALU.mult),
                 reads=[src_keys[c], 'rstd', 'vecs'], writes=[dst_keys[c]])

    def pre_sq(self, dc, T):
        S = self.S
        S.op('act', lambda a: a.activation(out=self.hT[:, dc, :T], in_=self.xT[:, dc, :T], func=AF.Square),
             reads=[('xT', dc)], writes=[('hT', dc)])

    def pre_mm(self, dc, T):
        S, ps = self.S, self.ps
        S.op('pe', lambda t: t.matmul(ps[:, 6, :T], lhsT=self.ones_mean[:], rhs=self.hT[:, dc, :T],
                                      start=(dc == 0), stop=(dc == KC - 1)),
             reads=[('hT', dc), 'ones_mean'], writes=[('ps', 6)])
        if dc == KC - 1:
            self.norm_pre = T

    def xT_src(self, T):
        return lambda c: self.xT[:, c, :T]

    def ffn(self, T, i):
        S, ps = self.S, self.ps
        l = i // 2
        gcol = VOFF[('nf1', l)] if i % 2 == 0 else VOFF[('nf2', l)]
        hT, act, xT = self.hT, self.big, self.xT
        xk = [('xT', c) for c in range(KC)]
        hk = [('hT', c) for c in range(KC)]
        self.rmsnorm(T, self.xT_src(T), xk, gcol, lambda c: hT[:, c, :T], hk)
        for fb in range(11):
            slot = self.wload('f1_%d' % i, fb)
            wv = self.wview(slot, [2, 8, 256])
            fbanks = [(self.mmbank(), self.mmbank()) for _ in range(2)]
            if fb == 0:
                for kc in range(KC):
                    for j in range(2):
                        for gu in range(2):
                            if (j, gu) == (1, 1):
                                continue
                            bank = fbanks[j][gu]
                            S.op('pe', lambda t, gu=gu, bank=bank, kc=kc, j=j, wv=wv: t.matmul(
                                ps[:, bank, :T], lhsT=wv((gu, kc), (j * 128, 128)), rhs=hT[:, kc, :T],
                                start=(kc == 0), stop=(kc == KC - 1)),
                                 reads=[('w', slot), ('hT', kc)], writes=[('ps', bank)])
            for j in range(2):
                fc = fb * 2 + j
                bg, bu = fbanks[j]
                if fb != 0 or j == 1:
                    for gu, bank in ((0, bg), (1, bu)):
                        if fb == 0 and gu == 0:
                            continue
                        for kc in range(KC):
                            S.op('pe', lambda t, gu=gu, bank=bank, kc=kc, j=j, wv=wv: t.matmul(
                                ps[:, bank, :T], lhsT=wv((gu, kc), (j * 128, 128)), rhs=hT[:, kc, :T],
                                start=(kc == 0), stop=(kc == KC - 1)),
                                 reads=[('w', slot), ('hT', kc)], writes=[('ps', bank)])
                ts = self.tmpslot()
                S.op('act', lambda a, bg=bg, ts=ts: a.activation(out=self.tmpf[:, ts, :T], in_=ps[:, bg, :T], func=AF.Silu),
                     reads=[('ps', bg)], writes=[('tmpf', ts)])
                S.op('dve', lambda v, bu=bu, ts=ts, fc=fc: v.tensor_tensor(out=act[:, fc * 512:fc * 512 + T], in0=self.tmpf[:, ts, :T],
                                                                             in1=ps[:, bu, :T], op=ALU.mult),
                     reads=[('ps', bu), ('tmpf', ts)], writes=[('big', fc)])
        S.op('act', lambda a: a.activation(out=self.vst[:, 38:39], in_=self.epsc[:, 0:1], func=AF.Ln), reads=['epsc'], writes=['dummy_ln'])
        for dc in range(8):
            slot = self.wload('f2_%d' % i, dc)
            wv = self.wview(slot, [22, 128])
            bank = self.outbank()
            for fc in range(NFC):
                S.op('pe', lambda t, fc=fc, bank=bank, wv=wv: t.matmul(
                    ps[:, bank, :T], lhsT=wv((fc,), (0, 128)), rhs=act[:, fc * 512:fc * 512 + T],
                    start=(fc == 0), stop=(fc == NFC - 1)),
                     reads=[('w', slot), ('big', fc)], writes=[('ps', bank)])
            if dc > 0:
                self.pre_mm(dc - 1, T)
            S.op('dve', lambda v, dc=dc, bank=bank: v.scalar_tensor_tensor(out=xT[:, dc, :T], in0=ps[:, bank, :T], scalar=0.5,
                                                                             in1=xT[:, dc, :T], op0=ALU.mult, op1=ALU.add),
                 reads=[('ps', bank), ('xT', dc)], writes=[('xT', dc)])
            self.pre_sq(dc, T)
        self.pre_mm(7, T)

    def gmlp(self, T, j, l):
        S, ps = self.S, self.ps
        hT, xT, big = self.hT, self.xT, self.big
        nsb = T // 128
        xk = [('xT', c) for c in range(KC)]
        hk = [('hT', c) for c in range(KC)]
        self.rmsnorm(T, self.xT_src(T), xk, VOFF[('nmx', l)], lambda c: hT[:, c, :T], hk)
        blsrc = dap(self.BL, j * 4 * AH, [(AH, 4), (1, AH)])
        brsrc = dap(self.BR, j * 4 * 4096, [(4096, 4), (1, 4096)])
        S.op('pool', lambda g: g.dma_start(out=self.bl[0:4, :], in_=blsrc), writes=['bl'], dma=True)
        S.op('pool', lambda g: g.dma_start(out=self.br[0:4, :], in_=brsrc), writes=['br'], dma=True)

        def uT(fc):
            return big[:, fc * 512:fc * 512 + T]

        VB0 = 24 * 512

        def vb(sb, f0, n):
            o = VB0 + sb * AH + f0
            return big[:, o:o + n]

        for pc in range(6):
            slot = self.wload('ain_%d' % j, 6 + pc)
            wv = self.wview(slot, [8, 512])
            vbanks = [self.mmbank() for _ in range(nsb)]
            nwave = min(nsb, 3) if pc == 0 else 0
            if pc == 0:
                for kc in range(KC):
                    for sb in range(nwave):
                        bank = vbanks[sb]
                        S.op('pe', lambda t, bank=bank, kc=kc, sb=sb, wv=wv: t.matmul(
                            ps[:, bank, :], lhsT=hT[:, kc, sb * 128:(sb + 1) * 128], rhs=wv((kc,), (0, 512)),
                            start=(kc == 0), stop=(kc == KC - 1)),
                             reads=[('w', slot), ('hT', kc)], writes=[('ps', bank)])
            for sb in range(nsb):
                bank = vbanks[sb]
                if sb >= nwave:
                    for kc in range(KC):
                        S.op('pe', lambda t, bank=bank, kc=kc, sb=sb, wv=wv: t.matmul(
                            ps[:, bank, :], lhsT=hT[:, kc, sb * 128:(sb + 1) * 128], rhs=wv((kc,), (0, 512)),
                            start=(kc == 0), stop=(kc == KC - 1)),
                             reads=[('w', slot), ('hT', kc)], writes=[('ps', bank)])
                S.op('act', lambda a, bank=bank, sb=sb, pc=pc: a.activation(out=vb(sb, pc * 512, 512), in_=ps[:, bank, :],
                                                                             func=AF.Gelu_apprx_tanh),
                     reads=[('ps', bank)], writes=[('vb', sb)])
                S.op('act', lambda a, sb=sb, pc=pc: a.activation(out=self.junk[:], in_=vb(sb, pc * 512, 512), func=AF.Square,
                                                                  accum_out=self.vst[:, sb * 6 + pc:sb * 6 + pc + 1]),
                     reads=[('vb', sb)], writes=['junk', ('vssq', sb)])
        S.op('act', lambda a: a.activation(out=self.vst[:, 39:40], in_=self.vst[:, 6 * nsb - 1:6 * nsb], func=AF.Copy),
             reads=[('vssq', sb) for sb in range(nsb)], writes=['vfence'])
        S.op('dve', lambda v: v.tensor_reduce(out=self.vst[:, 24:24 + nsb], in_=self.vst[:, 0:6 * nsb].rearrange("p (a b) -> p a b", b=6),
                                              axis=mybir.AxisListType.X, op=ALU.add),
             reads=[('vssq', sb) for sb in range(nsb)] + ['vfence'], writes=['vsum'])
        S.op('act', lambda a: a.activation(out=self.vst[:, 28:28 + nsb], in_=self.vst[:, 24:24 + nsb], func=AF.Ln,
                                           bias=self.epsc[:, 0:1], scale=1.0 / AH),
             reads=['vsum', 'epsc'], writes=['vsq'])
        S.op('act', lambda a: a.activation(out=self.vst[:, 32:32 + nsb], in_=self.vst[:, 28:28 + nsb], func=AF.Exp, scale=-0.5),
             reads=['vsq'], writes=['vrstd'])
        for sb in range(nsb):
            S.op('dve', lambda v, sb=sb: v.tensor_scalar(out=self.wsts[:, sb, :], in0=self.wst32[:, j, :, :].rearrange("p a b -> p (a b)"),
                                                           scalar1=self.vst[:, 32 + sb:33 + sb], scalar2=None, op0=ALU.mult),
                 reads=['vrstd', 'wst32'], writes=[('wsts', sb)])
        for pc in range(6):
            slot = self.wload('ain_%d' % j, pc)
            wv = self.wview(slot, [8, 512])
            for jj in range(4):
                fc = pc * 4 + jj
                bank = self.mmbank()
                for kc in range(KC):
                    S.op('pe', lambda t, bank=bank, kc=kc, jj=jj, wv=wv: t.matmul(
                        ps[:, bank, :T], lhsT=wv((kc,), (jj * 128, 128)), rhs=hT[:, kc, :T],
                        start=(kc == 0), stop=(kc == KC - 1)),
                         reads=[('w', slot), ('hT', kc)], writes=[('ps', bank)])
                S.op('act', lambda a, bank=bank, fc=fc: a.activation(out=uT(fc), in_=ps[:, bank, :T], func=AF.Gelu_apprx_tanh),
                     reads=[('ps', bank)], writes=[('big', fc)])
        gvc = VOFF[('agv', j)]
        for fc in range(24):
            g = fc // 3
            bank = self.mmbank()
            S.op('pe', lambda t, bank=bank, fc=fc, g=g: t.matmul(
                ps[:, bank, :T], lhsT=self.bl[:, fc * 128:(fc + 1) * 128], rhs=self.br[:, g * 512:g * 512 + T],
                start=True, stop=False),
                 reads=['bl', 'br'], writes=[('ps', bank)])
            for sb in range(nsb):
                S.op('pe', lambda t, bank=bank, sb=sb, fc=fc, g=g: t.matmul(
                    ps[:, bank, sb * 128:(sb + 1) * 128], lhsT=vb(sb, fc * 128, 128),
                    rhs=self.wsts[:, sb, g * 128:(g + 1) * 128], start=False, stop=(sb == nsb - 1)),
                     reads=[('vb', sb), ('wsts', sb)], writes=[('ps', bank)])
            S.op('dve', lambda v, bank=bank, fc=fc: v.scalar_tensor_tensor(
                out=uT(fc), in0=ps[:, bank, :T], scalar=self.vecs[:, gvc + fc:gvc + fc + 1], in1=uT(fc),
                op0=ALU.mult, op1=ALU.mult),
                 reads=[('ps', bank), ('big', fc), 'vecs'], writes=[('big', fc)])
        for dc in range(8):
            slot = self.wload('aout_%d' % j, dc)
            wv = self.wview(slot, [24, 128])
            bank = self.outbank()
            for fc in range(24):
                S.op('pe', lambda t, fc=fc, bank=bank, wv=wv: t.matmul(
                    ps[:, bank, :T], lhsT=wv((fc,), (0, 128)), rhs=uT(fc), start=(fc == 0), stop=(fc == 23)),
                     reads=[('w', slot), ('big', fc)], writes=[('ps', bank)])
            if dc > 0:
                self.pre_mm(dc - 1, T)
            S.op('dve', lambda v, dc=dc, bank=bank: v.tensor_tensor(out=xT[:, dc, :T], in0=ps[:, bank, :T], in1=xT[:, dc, :T], op=ALU.add),
                 reads=[('ps', bank), ('xT', dc)], writes=[('xT', dc)])
            self.pre_sq(dc, T)
        self.pre_mm(7, T)

    def fm_io(self, dram, nchunk, sb_tile, pieces, store, keys, q='pool'):
        S = self.S
        for tok0, n, col0 in pieces:
            d = dap(dram, tok0, [(NL, 128), (128 * NL, nchunk), (1, n)])
            s = sb_tile[:, 0:nchunk, col0:col0 + n]
            if store:
                S.op(q, lambda e, d=d, s=s: e.dma_start(out=d, in_=s), reads=keys, dma=True)
            else:
                S.op(q, lambda e, d=d, s=s: e.dma_start(out=s, in_=d), writes=keys, dma=True)

    def phaseA(self, tiles):
        S, nc, ps = self.S, self.nc, self.ps
        xT, hT = self.xT, self.hT
        xk = [('xT', c) for c in range(KC)]
        hk = [('hT', c) for c in range(KC)]
        nrest = sum(len(v) for v in self.convq.values())
        per = (nrest + 5) // 6
        self.phaseA_xload(*tiles[0])
        for g in self.phaseA_tgroups(*tiles[0]):
            g()
        for ti, (tok0, T) in enumerate(tiles):
            nxt = tiles[ti + 1] if ti + 1 < len(tiles) else None
            self.phaseA_tile(tok0, T, nxt)
            self.emit_conv(n=per)
        self.emit_conv(n=100000)

    def phaseA_xload(self, tok0, T):
        S = self.S
        for sb in range(T // 128):
            xb = sb % 4
            src = dap(self.xl, (tok0 + sb * 128) * D, [(D, 128), (1, D)])
            S.op('sp', lambda g, xb=xb, src=src: g.dma_start(out=self.xin[:, xb, :], in_=src),
                 writes=[('xin', xb)], dma=True)

    def phaseA_tgroups(self, tok0, T):
        S, ps, xT = self.S, self.ps, self.xT
        groups = []
        for sb in range(T // 128):
            xb = sb % 4
            for half in range(2):
                def grp(sb=sb, xb=xb, half=half):
                    bank = 6 + half
                    for cc in range(4):
                        c = half * 4 + cc
                        S.op('pe', lambda t, bank=bank, cc=cc, c=c, xb=xb: t.transpose(
                            ps[:, bank, cc * 128:(cc + 1) * 128], self.xin[:, xb, c * 128:(c + 1) * 128], self.ident[:]),
                             reads=[('xin', xb), 'ident'], writes=[('ps', bank)])
                    S.op('act', lambda a, bank=bank, half=half, sb=sb: a.activation(
                        out=xT[:, half * 4:half * 4 + 4, sb * 128:(sb + 1) * 128],
                        in_=ps[:, bank, :].rearrange("p (a b) -> p a b", b=128), func=AF.Copy),
                         reads=[('ps', bank)], writes=[('xT', half * 4 + cc) for cc in range(4)])
                groups.append(grp)
        return groups

    def phaseA_tile(self, tok0, T, nxt=None):
        S, nc, ps = self.S, self.nc, self.ps
        xT, hT = self.xT, self.hT
        xk = [('xT', c) for c in range(KC)]
        hk = [('hT', c) for c in range(KC)]
        if True:
            nsb = T // 128
            self.ffn(T, 0)
            self.gmlp(T, 0, 0)
            self.ffn(T, 1)
            if nxt is not None:
                self.phaseA_xload(*nxt)
            self.ffn(T, 2)
            need_q = not (tok0 >= NP_ and (tok0 + T <= NP_ + HALO - QH or tok0 >= NP_ + HALO + OWN + QH))
            if need_q:
                self.fm_io(self.XS1, 8, xT, [(tok0, T, 0)], True, xk)
            self.rmsnorm(T, self.xT_src(T), xk, VOFF[('nmx', 1)], lambda c: hT[:, c, :T], hk)
            qk = self.qk
            tgroups = self.phaseA_tgroups(*nxt) if nxt is not None else []
            for pc in range(6):
                if pc < 3 and not need_q:
                    continue
                slot = self.wload('qkv', pc)
                wv = self.wview(slot, [8, 384])
                qbanks = [self.mmbank() for _ in range(3)]
                first_wave = (pc == (0 if need_q else 3))
                if first_wave:
                    for kc in range(KC):
                        for jj in range(3):
                            bank = qbanks[jj]
                            S.op('pe', lambda t, bank=bank, kc=kc, jj=jj, wv=wv: t.matmul(
                                ps[:, bank, :T], lhsT=wv((kc,), (jj * 128, 128)), rhs=hT[:, kc, :T],
                                start=(kc == 0), stop=(kc == KC - 1)),
                                 reads=[('w', slot), ('hT', kc)], writes=[('ps', bank)])
                for jj in range(3):
                    c = pc * 3 + jj
                    bank = qbanks[jj]
                    if not first_wave:
                        for kc in range(KC):
                            S.op('pe', lambda t, bank=bank, kc=kc, jj=jj, wv=wv: t.matmul(
                                ps[:, bank, :T], lhsT=wv((kc,), (jj * 128, 128)), rhs=hT[:, kc, :T],
                                start=(kc == 0), stop=(kc == KC - 1)),
                                 reads=[('w', slot), ('hT', kc)], writes=[('ps', bank)])
                    sc = 0.125 if c < 9 else 1.0
                    S.op('act', lambda a, bank=bank, c=c, sc=sc: a.activation(out=qk[:, c, :T], in_=ps[:, bank, :T], func=AF.Copy, scale=sc),
                         reads=[('ps', bank)], writes=[('qk', c)])
                    if tgroups:
                        tgroups.pop(0)()
            while tgroups:
                tgroups.pop(0)()
            if need_q:
                self.fm_io(self.QT, 9, qk, [(tok0, T, 0)], True, [('qk', c) for c in range(9)])
            dK = dap(self.KT, tok0, [(NL, 128), (128 * NL, 9), (1, T)])
            S.op('pool', lambda e, dK=dK, T=T: e.dma_start(out=dK, in_=qk[:, 9:18, 0:T]),
                 reads=[('qk', c) for c in range(9, 18)], dma=True)
            vstg = self.vstg
            for g in range(3):
                slot = self.wload('qkv', 6 + g)
                wv = self.wview(slot, [8, 384])
                for sb in range(nsb):
                    bank = self.mmbank()
                    blk = (tok0 + sb * 128) // 128
                    for kc in range(KC):
                        S.op('pe', lambda t, bank=bank, kc=kc, sb=sb, wv=wv: t.matmul(
                            ps[:, bank, 0:384], lhsT=hT[:, kc, sb * 128:(sb + 1) * 128], rhs=wv((kc,), (0, 384)),
                            start=(kc == 0), stop=(kc == KC - 1)),
                             reads=[('w', slot), ('hT', kc)], writes=[('ps', bank)])
                    S.op('act', lambda a, bank=bank, sb=sb, g=g, blk=blk: a.activation(
                        out=vstg[:, sb, g * 384:(g + 1) * 384], in_=ps[:, bank, 0:384], func=AF.Identity,
                        scale=self.validcol[:, blk:blk + 1]),
                         reads=[('ps', bank), 'validcol'], writes=[('vstg', sb)])
            for sb in range(nsb):
                blk = (tok0 + sb * 128) // 128
                S.op('act', lambda a, sb=sb, blk=blk: a.activation(out=vstg[:, sb, 1152:VW], in_=self.ones_bf[:], func=AF.Identity,
                                                                    scale=self.validcol[:, blk:blk + 1]),
                     reads=['ones_bf', 'validcol'], writes=[('vstg', sb)])
            dV = dap(self.VV, tok0 * VW, [(VW, 128), (128 * VW, nsb), (1, VW)])
            S.op('pool', lambda e, dV=dV, nsb=nsb: e.dma_start(out=dV, in_=vstg[:, 0:nsb, :]),
                 reads=[('vstg', sb) for sb in range(nsb)], dma=True)

    def phaseB0(self, st):
        S, ps = self.S, self.ps
        kbuf, qbuf, E0, E1 = st['kbuf'], st['qbuf'], st['E0'], st['E1']
        numbuf, denbuf, ostage = st['numbuf'], st['denbuf'], st['ostage']
        S.op('sp', lambda q: q.dma_start(out=E0[:], in_=self.braw0_d.ap()), writes=['E0'], dma=True)
        S.op('sp', lambda q: q.dma_start(out=E1[:], in_=self.braw1_d.ap()), writes=['E1'], dma=True)
        S.op('act', lambda a: a.activation(out=E0[:], in_=E0[:], func=AF.Exp), reads=['E0'], writes=['E0'])
        S.op('act', lambda a: a.activation(out=E1[:], in_=E1[:], func=AF.Exp), reads=['E1'], writes=['E1'])
        segs = [(0, NP_, 0, NP_), (NP_, NP_ + NS, NP_ + HALO - QH, NP_ + HALO + OWN + QH)]
        un = 0
        ld = 0
        vn = 0
        vprev = None
        for hp in range(3):
            for (slo, shi, qlo, qhi) in segs:
                NQ = qhi - qlo
                NK = shi - slo
                pend = None
                for g, dil in enumerate(DILS):
                    c = g * 3 + hp
                    kb_i = ld % 2
                    ld += 1
                    dk = dap(self.KT, c * 128 * NL + slo, [(NL, 128), (1, NK)])
                    dq = dap(self.QT, c * 128 * NL + qlo, [(NL, 128), (1, NQ)])
                    S.op('sp', lambda e, dk=dk, kb_i=kb_i, NK=NK: e.dma_start(out=kbuf[:, kb_i, 0:NK], in_=dk),
                         writes=[('kbuf', kb_i)], dma=True)
                    S.op('sp', lambda e, dq=dq, kb_i=kb_i, NQ=NQ: e.dma_start(out=qbuf[:, kb_i, 0:NQ], in_=dq),
                         writes=[('qbuf', kb_i)], dma=True)
                    Lr = NK // dil
                    i_lo = (qlo - slo) // dil
                    nqr = NQ // dil
                    gh0 = g * 6 + 2 * hp
                    for r in range(dil):
                        for a in range(i_lo, i_lo + nqr, 128):
                            nq = min(128, i_lo + nqr - a)
                            qcol0 = (a - i_lo) * dil + r
                            blocks = []
                            if a - 64 < 0:
                                assert a == 0
                                blocks.append((0, min(64, Lr), 1, 0))
                            else:
                                blocks.append((a - 64, min(128, Lr - (a - 64)), 0, 128))
                            nkb = min(nq, Lr - (a + 64))
                            if nkb > 0:
                                blocks.append((a + 64, nkb, 0, 0))
                            par = un % 2
                            un += 1
                            binfo = []
                            for bi, (kstart, nk, tab, cc0) in enumerate(blocks):
                                key_v = (hp, slo, g, r, kstart, nk)
                                if bi == 0 and vprev is not None and vprev[0] == key_v:
                                    vslot, need = vprev[1], False
                                else:
                                    vslot, need = vn % 8, True
                                    vn += 1
                                binfo.append((bi, kstart, nk, tab, cc0, vslot, need))
                                if bi == 1:
                                    vprev = (key_v, vslot)
                            if len(blocks) < 2:
                                vprev = None
                            self._b0_stage1(st, binfo, par, slo, r, dil, c, kb_i, qcol0, nq, gh0)
                            if pend is not None:
                                pend()
                            pend = (lambda binfo=binfo, par=par, nq=nq, g=g, qcol0=qcol0, dil=dil:
                                    self._b0_stage2(st, binfo, par, nq, g, qcol0, dil))
                if pend is not None:
                    pend()
                    pend = None
                S.op('dve', lambda v, NQ=NQ: v.reciprocal(out=denbuf[:, 0:NQ], in_=denbuf[:, 0:NQ]), reads=['den'], writes=['den'])
                for g in range(3):
                    c = g * 3 + hp
                    S.op('dve', lambda v, g=g, NQ=NQ: v.tensor_tensor(out=ostage[:, 0:NQ], in0=numbuf[:, g, 0:NQ], in1=denbuf[:, 0:NQ],
                                                                     op=ALU.mult),
                         reads=[('num', g), 'den'], writes=['ostage'])
                    do = dap(self.OT, c * 128 * NL + qlo, [(NL, 128), (1, NQ)])
                    S.op('pool', lambda e, do=do, NQ=NQ: e.dma_start(out=do, in_=ostage[:, 0:NQ]), reads=['ostage'], dma=True)

    def _b0_stage1(self, st, binfo, par, slo, r, dil, c, kb_i, qcol0, nq, gh0):
        S, ps = self.S, self.ps
        kbuf, qbuf, E0, E1 = st['kbuf'], st['qbuf'], st['E0'], st['E1']
        vblk, es_t, pb_t = st['vblk'], st['es'], st['pb']
        sb0 = 0 if par == 0 else 6
        for (bi, kstart, nk, tab, cc0, vslot, need) in binfo:
            if not need:
                continue
            tok = slo + r + dil * kstart
            dv = dap(self.VV, tok * VW + c * 128, [(dil * VW, nk), (1152 - c * 128, 2), (1, 128)])
            S.op('pool', lambda e, dv=dv, vslot=vslot, nk=nk: e.dma_start(out=vblk[0:nk, vslot, :, :], in_=dv),
                 writes=[('vblk', vslot)], dma=True)
        for (bi, kstart, nk, tab, cc0, vslot, need) in binfo:
            kc0 = r + dil * kstart
            for h in range(2):
                S.op('pe', lambda t, h=h, kc0=kc0, nk=nk, bi=bi: t.matmul(
                    ps[0:nk, sb0 + h, bi * 128: bi * 128 + nq],
                    lhsT=kbuf[h * 64:(h + 1) * 64, kb_i, kc0:kc0 + dil * (nk - 1) + 1:dil],
                    rhs=qbuf[h * 64:(h + 1) * 64, kb_i, qcol0:qcol0 + dil * (nq - 1) + 1:dil],
                    start=True, stop=True),
                     reads=[('kbuf', kb_i), ('qbuf', kb_i)], writes=[('pss', par)])
        for (bi, kstart, nk, tab, cc0, vslot, need) in binfo:
            sview = ps[0:nk, sb0:sb0 + 2, bi * 128:bi * 128 + nq]
            S.op('act', lambda a_, sview=sview, nk=nk, bi=bi: a_.activation(
                out=es_t[0:nk, par, bi, :, 0:nq], in_=sview, func=AF.Exp),
                 reads=[('pss', par)], writes=[('es', par, bi)])
            if tab == 0:
                ev = E0[0:nk, gh0 * 256:(gh0 + 2) * 256].rearrange("p (a b) -> p a b", b=256)[:, :, cc0:cc0 + nq]
            else:
                ev = E1[0:nk, gh0 * 128:(gh0 + 2) * 128].rearrange("p (a b) -> p a b", b=128)[:, :, 0:nq]
            S.op('dve', lambda v, ev=ev, nk=nk, bi=bi: v.tensor_tensor(
                out=pb_t[0:nk, par, bi, :, 0:nq], in0=es_t[0:nk, par, bi, :, 0:nq], in1=ev, op=ALU.mult),
                 reads=[('es', par, bi), 'E0', 'E1'], writes=[('pb', par, bi)])

    def _b0_stage2(self, st, binfo, par, nq, g, qcol0, dil):
        S, ps = self.S, self.ps
        numbuf, denbuf = st['numbuf'], st['denbuf']
        vblk, pb_t = st['vblk'], st['pb']
        nb = len(binfo)
        for (bi, kstart, nk, tab, cc0, vslot, need) in binfo:
            S.op('pe', lambda t, vslot=vslot, nk=nk, bi=bi: t.matmul(
                ps[:, 2 + par, 0:256].rearrange("p (a b) -> p a b", b=128)[:, :, 0:nq],
                lhsT=vblk[0:nk, vslot, 0, :], rhs=pb_t[0:nk, par, bi, :, 0:nq],
                start=(bi == 0), stop=(bi == nb - 1)),
                 reads=[('vblk', vslot), ('pb', par, bi)], writes=[('psn', par)])
        for (bi, kstart, nk, tab, cc0, vslot, need) in binfo:
            S.op('pe', lambda t, nk=nk, bi=bi, vslot=vslot: t.matmul(
                ps[:, 4 + par, 0:256].rearrange("p (a b) -> p a b", b=128)[:, :, 0:nq],
                lhsT=vblk[0:nk, vslot, 1, :], rhs=pb_t[0:nk, par, bi, :, 0:nq],
                start=(bi == 0), stop=(bi == nb - 1)),
                 reads=[('vblk', vslot), ('pb', par, bi)], writes=[('psd', par)])
        qsl = slice(qcol0, qcol0 + dil * (nq - 1) + 1, dil)
        for h in range(2):
            S.op('act', lambda a_, h=h: a_.activation(
                out=numbuf[h * 64:(h + 1) * 64, g, qsl], in_=ps[h * 64:(h + 1) * 64, 2 + par, h * 128:h * 128 + nq],
                func=AF.Copy),
                 reads=[('psn', par)], writes=[('num', g)])
            if g == 0:
                S.op('dve', lambda v, h=h: v.tensor_copy(
                    out=denbuf[h * 64:(h + 1) * 64, qsl], in_=ps[h * 64:(h + 1) * 64, 4 + par, h * 128:h * 128 + nq]),
                     reads=[('psd', par)], writes=['den'])
            else:
                S.op('dve', lambda v, h=h: v.tensor_tensor(
                    out=denbuf[h * 64:(h + 1) * 64, qsl], in0=ps[h * 64:(h + 1) * 64, 4 + par, h * 128:h * 128 + nq],
                    in1=denbuf[h * 64:(h + 1) * 64, qsl], op=ALU.add),
                     reads=[('psd', par), 'den'], writes=['den'])

    def phaseB1(self, tiles, st):
        S, ps = self.S, self.ps
        xT, hT = self.xT, self.hT
        oT, vmask, gT = st['oT'], st['vmask'], st['gT']
        xk = [('xT', c) for c in range(KC)]
        hk = [('hT', c) for c in range(KC)]
        ok = [('oT', c) for c in range(9)]
        self.fm_io(self.XS1, 8, xT, tiles[0], False, xk)
        self.fm_io(self.OT, 9, oT, tiles[0], False, ok)
        for ti, pieces in enumerate(tiles):
            T = sum(n for _, n, _ in pieces)
            for tok0, n, col0 in pieces:
                d = dap(self.validrow_d, tok0, [(0, 128), (1, n)])
                S.op('pool', lambda e, d=d, col0=col0, n=n: e.dma_start(out=vmask[:, col0:col0 + n], in_=d),
                     writes=['vmask'], dma=True)
            for dc in range(8):
                slot = self.wload('wo', dc)
                wv = self.wview(slot, [9, 128])
                bank = self.outbank()
                for c in range(9):
                    S.op('pe', lambda t, c=c, bank=bank, wv=wv, T=T: t.matmul(
                        ps[:, bank, :T], lhsT=wv((c,), (0, 128)), rhs=oT[:, c, :T], start=(c == 0), stop=(c == 8)),
                         reads=[('w', slot), ('oT', c)], writes=[('ps', bank)])
                if dc > 0:
                    self.pre_mm(dc - 1, T)
                S.op('dve', lambda v, dc=dc, bank=bank, T=T: v.tensor_tensor(out=xT[:, dc, :T], in0=ps[:, bank, :T], in1=xT[:, dc, :T], op=ALU.add),
                     reads=[('ps', bank), ('xT', dc)], writes=[('xT', dc)])
                self.pre_sq(dc, T)
            self.pre_mm(7, T)
            if ti + 1 < len(tiles):
                self.fm_io(self.OT, 9, oT, tiles[ti + 1], False, ok)
            self.ffn(T, 3)
            self.ffn(T, 4)
            self.fm_io(self.XS2, 8, xT, pieces, True, xk)
            self.rmsnorm(T, self.xT_src(T), xk, VOFF[('nmx', 2)], lambda c, T=T: hT[:, c, :T], hk)
            if ti + 1 < len(tiles):
                self.fm_io(self.XS1, 8, xT, tiles[ti + 1], False, xk)
            cb1 = VOFF['cb1']
            for fb in range(4):
                slot = self.wload('pw1', fb)
                wv = self.wview(slot, [2, 8, 256])
                pbanks = [(self.mmbank(), self.mmbank()) for _ in range(2)]
                if fb == 0:
                    for kc in range(KC):
                        for jj in range(2):
                            for ag in range(2):
                                if (jj, ag) == (1, 1):
                                    continue
                                bank = pbanks[jj][ag]
                                S.op('pe', lambda t, ag=ag, bank=bank, kc=kc, jj=jj, wv=wv, T=T: t.matmul(
                                    ps[:, bank, :T], lhsT=wv((ag, kc), (jj * 128, 128)), rhs=hT[:, kc, :T],
                                    start=(kc == 0), stop=(kc == KC - 1)),
                                     reads=[('w', slot), ('hT', kc)], writes=[('ps', bank)])
                for jj in range(2):
                    c = fb * 2 + jj
                    ba, bg = pbanks[jj]
                    if fb != 0 or jj == 1:
                        for ag, bank in ((0, ba), (1, bg)):
                            if fb == 0 and ag == 0:
                                continue
                            for kc in range(KC):
                                S.op('pe', lambda t, ag=ag, bank=bank, kc=kc, jj=jj, wv=wv, T=T: t.matmul(
                                    ps[:, bank, :T], lhsT=wv((ag, kc), (jj * 128, 128)), rhs=hT[:, kc, :T],
                                    start=(kc == 0), stop=(kc == KC - 1)),
                                     reads=[('w', slot), ('hT', kc)], writes=[('ps', bank)])
                    ts = self.tmpslot()
                    S.op('act', lambda a, bg=bg, ts=ts, c=c, T=T: a.activation(out=self.tmpf[:, ts, :T], in_=ps[:, bg, :T], func=AF.Sigmoid,
                                                                                 bias=self.vecs[:, cb1 + 8 + c:cb1 + 9 + c]),
                         reads=[('ps', bg), 'vecs'], writes=[('tmpf', ts)])
                    S.op('dve', lambda v, ba=ba, ts=ts, c=c, T=T: v.scalar_tensor_tensor(
                        out=gT[:, c, :T], in0=ps[:, ba, :T], scalar=self.vecs[:, cb1 + c:cb1 + c + 1], in1=self.tmpf[:, ts, :T],
                        op0=ALU.add, op1=ALU.mult),
                         reads=[('ps', ba), ('tmpf', ts), 'vecs'], writes=[('gT', c)])
                    S.op('dve', lambda v, c=c, T=T: v.tensor_tensor(out=gT[:, c, :T], in0=gT[:, c, :T], in1=vmask[:, :T], op=ALU.mult),
                         reads=[('gT', c), 'vmask'], writes=[('gT', c)])
            self.fm_io(self.GT, 8, gT, pieces, True, [('gT', c) for c in range(8)])

    def phaseC(self, tiles, st):
        S, ps = self.S, self.ps
        xT, hT = self.xT, self.hT
        gpad, cacc = st['gpad'], st['cacc']
        xk = [('xT', c) for c in range(KC)]
        hk = [('hT', c) for c in range(KC)]
        gk = [('gpad', c) for c in range(KC)]
        ck = [('cacc', c) for c in range(KC)]
        cdw, cbdw, cgn, cb2 = VOFF['cdw'], VOFF['cbdw'], VOFF['cgn'], VOFF['cb2']
        T = TT
        def load_gpad(tile):
            tok0, seg_lo, seg_hi, out0 = tile
            lo = max(tok0 - 15, seg_lo)
            hi = min(tok0 + T + 15, seg_hi)
            c0 = lo - (tok0 - 15)
            n = hi - lo
            if c0 > 0:
                S.op('dve', lambda v, c0=c0: v.memset(gpad[:, :, 0:c0], 0.0), writes=gk)
            if c0 + n < T + 30:
                S.op('dve', lambda v, c0=c0, n=n: v.memset(gpad[:, :, c0 + n:T + 30], 0.0), writes=gk)
            self.fm_io(self.GT, 8, gpad, [(lo, n, c0)], False, gk)

        load_gpad(tiles[0])
        for ti, (tok0, seg_lo, seg_hi, out0) in enumerate(tiles):
            self.fm_io(self.XS2, 8, xT, [(tok0, T, 0)], False, xk)
            for c in range(KC):
                slot = self.wload('dwd', c)
                wv = self.wview(slot, [31, 128])
                bank = self.outbank()
                for j in range(31):
                    S.op('pe', lambda t, c=c, j=j, bank=bank, wv=wv: t.matmul(
                        ps[:, bank, :], lhsT=wv((j,), (0, 128)), rhs=gpad[:, c, j:j + T], start=(j == 0), stop=(j == 30)),
                         reads=[('w', slot), ('gpad', c)], writes=[('ps', bank)])
                S.op('act', lambda a_, c=c, bank=bank: a_.activation(out=cacc[:, c, :], in_=ps[:, bank, :], func=AF.Identity,
                                                                       bias=self.vecs[:, cbdw + c:cbdw + c + 1]),
                     reads=[('ps', bank), 'vecs'], writes=[('cacc', c)])
            if ti + 1 < len(tiles):
                load_gpad(tiles[ti + 1])
            self.rmsnorm(T, lambda c: cacc[:, c, :], ck, cgn, lambda c: cacc[:, c, :], ck)
            for c in range(KC):
                S.op('act', lambda a, c=c: a.activation(out=hT[:, c, :], in_=cacc[:, c, :], func=AF.Silu),
                     reads=[('cacc', c)], writes=[('hT', c)])
            for dc in range(8):
                slot = self.wload('pw2', dc)
                wv = self.wview(slot, [8, 128])
                bank = self.outbank()
                for kc in range(KC):
                    S.op('pe', lambda t, kc=kc, bank=bank, wv=wv: t.matmul(
                        ps[:, bank, :], lhsT=wv((kc,), (0, 128)), rhs=hT[:, kc, :], start=(kc == 0), stop=(kc == KC - 1)),
                         reads=[('w', slot), ('hT', kc)], writes=[('ps', bank)])
                S.op('dve', lambda v, dc=dc, bank=bank: v.scalar_tensor_tensor(
                    out=xT[:, dc, :], in0=ps[:, bank, :], scalar=self.vecs[:, cb2 + dc:cb2 + dc + 1], in1=xT[:, dc, :],
                    op0=ALU.add, op1=ALU.add),
                     reads=[('ps', bank), ('xT', dc), 'vecs'], writes=[('xT', dc)])
            self.ffn(T, 5)
            self.ffn(T, 6)
            self.gmlp(T, 1, 3)
            self.ffn(T, 7)
            self.rmsnorm(T, self.xT_src(T), xk, VOFF['nfin'], lambda c: cacc[:, c, :], ck)
            for sb in range(T // 128):
                xb = sb % 2
                for half in range(2):
                    bank = 6 + half
                    for cc in range(4):
                        c = half * 4 + cc
                        S.op('pe', lambda t, bank=bank, cc=cc, c=c, sb=sb: t.transpose(
                            ps[:, bank, cc * 128:(cc + 1) * 128], cacc[:, c, sb * 128:(sb + 1) * 128], self.ident[:]),
                             reads=[('cacc', c), 'ident'], writes=[('ps', bank)])
                    S.op('act', lambda a, bank=bank, half=half, xb=xb: a.activation(
                        out=self.xin[:, xb, half * 512:(half + 1) * 512], in_=ps[:, bank, :], func=AF.Copy),
                         reads=[('ps', bank)], writes=[('xin', xb)])
                dst = dap(self.y, (out0 + sb * 128) * D, [(D, 128), (1, D)])
                S.op('pool', lambda e, dst=dst, xb=xb: e.dma_start(out=dst, in_=self.xin[:, xb, :]), reads=[('xin', xb)], dma=True)

    def build(self):
        nc, es = self.nc, self.es
        with es:
            self.declare()
            self.S = Sched(nc, es)
            S = self.S
            self.load_consts()
            self.epsc = es.enter_context(nc.sbuf_tensor("epsc", [128, 1], F32))
            S.op('dve', lambda v: v.memset(self.epsc[:], EPS), writes=['epsc'])
            with nc.sbuf_tensor("wsin", [128, 2, 128], F32) as wsin:
                self.load_wst(wsin)
                self.build_bias_rows()
                self.phase0()
                self.conv_rest = ['wo', 'f1_3', 'f2_3', 'f1_4', 'f2_4', 'pw1', 'pw2', 'f1_5', 'f2_5', 'f1_6', 'f2_6',
                                  'ain_1', 'aout_1', 'f1_7', 'f2_7']
                if self.stop_after == 0:
                    self.emit_conv(n=100000)
                S.flush(skip_pool_prefix=True)
            if self.stop_after == 0:
                S.flush(final=True)
                return nc
            def alloc_common(e2, sfx):
                self.xT = e2.enter_context(nc.sbuf_tensor("xT" + sfx, [128, 8, TT], F32))
                self.hT = e2.enter_context(nc.sbuf_tensor("hT" + sfx, [128, 8, TT], BF16))
                self.rt = e2.enter_context(nc.sbuf_tensor("rt" + sfx, [128, 2, TT], F32))
                self.big = e2.enter_context(nc.sbuf_tensor("big" + sfx, [128, 48 * TT], BF16))
                self.tmpf = e2.enter_context(nc.sbuf_tensor("tmpf" + sfx, [128, 2, TT], F32))
                self.wsts = e2.enter_context(nc.sbuf_tensor("wsts" + sfx, [128, 4, 1024], BF16))
                self.bl = e2.enter_context(nc.sbuf_tensor("bl" + sfx, [128, AH], BF16))
                self.br = e2.enter_context(nc.sbuf_tensor("br" + sfx, [128, 4096], BF16))
                S.op('dve', lambda v: v.memset(self.bl[:], 0.0), writes=['bl'])
                S.op('dve', lambda v: v.memset(self.br[:], 0.0), writes=['br'])
                self.vst = e2.enter_context(nc.sbuf_tensor("vstat" + sfx, [128, 40], F32))
                self.junk = e2.enter_context(nc.sbuf_tensor("junk" + sfx, [128, 512], BF16))
                self.wring = e2.enter_context(nc.sbuf_tensor("wring" + sfx, [128, NSLOT, SLOT_ELEMS], BF16))
                self.xin = e2.enter_context(nc.sbuf_tensor("xin" + sfx, [128, 4, D], F32))
            tilesA = [(t, TT) for t in range(0, NP_, TT)]
            t = NP_
            while t < NL:
                T = min(TT, NL - t)
                tilesA.append((t, T))
                t += T
            if self.dbg and 'tilesA' in self.dbg:
                tilesA = self.dbg['tilesA']
            with ExitStack() as e2:
                alloc_common(e2, '_a')
                qk = e2.enter_context(nc.sbuf_tensor("qk", [128, 18, TT], BF16))
                vstg = e2.enter_context(nc.sbuf_tensor("vstg", [128, 4, VW], BF16))
                self.qk, self.vstg = qk, vstg
                first = ['f1_0', 'f2_0', 'ain_0', 'aout_0', 'f1_1', 'f2_1', 'f1_2', 'f2_2', 'qkv']
                self.emit_conv(streams=first)
                self.phaseA(tilesA)
                S.flush()
            if self.stop_after == 'A':
                S.flush(final=True)
                return nc
            NQM = OWN + 2 * QH
            with nc.sbuf_tensor("kbuf", [128, 2, NS], BF16) as kbuf, \
                    nc.sbuf_tensor("qbuf", [128, 2, NQM], BF16) as qbuf, \
                    nc.sbuf_tensor("E0", [128, 18 * 256], F32) as E0, \
                    nc.sbuf_tensor("E1", [128, 18 * 128], F32) as E1, \
                    nc.sbuf_tensor("numbuf", [128, 3, NQM], F32) as numbuf, \
                    nc.sbuf_tensor("denbuf", [128, NQM], F32) as denbuf, \
                    nc.sbuf_tensor("ostage", [128, NQM], BF16) as ostage, \
                    nc.sbuf_tensor("vblk", [128, 8, 2, 128], BF16) as vblk, \
                    nc.sbuf_tensor("es_t", [128, 2, 2, 2, 128], F32) as es_t, \
                    nc.sbuf_tensor("pb_t", [128, 2, 2, 2, 128], BF16) as pb_t, \
                    nc.sbuf_tensor("dg", [128, 2, 3968], BF16) as dg:
                cdw = VOFF['cdw']
                for c in range(8):
                    for j in range(31):
                        S.op('dve', lambda v, c=c, j=j: v.tensor_scalar(out=dg[:, c % 2, j * 128:(j + 1) * 128], in0=self.ident[:],
                                                                         scalar1=self.vecs[:, cdw + j * 8 + c:cdw + j * 8 + c + 1],
                                                                         scalar2=None, op0=ALU.mult),
                             reads=['ident', 'vecs'], writes=[('dg', c % 2)])
                    dd = dap(self.wbf['dwd'], c * 128 * 3968, [(3968, 128), (1, 3968)])
                    S.op('sp', lambda q, dd=dd, c=c: q.dma_start(out=dd, in_=dg[:, c % 2, :]), reads=[('dg', c % 2)], writes=[('dwd', c, 0)], dma=True)
                self.phaseB0(dict(kbuf=kbuf, qbuf=qbuf, E0=E0, E1=E1, numbuf=numbuf, denbuf=denbuf, ostage=ostage,
                                  vblk=vblk, es=es_t, pb=pb_t))
                S.flush()
            if self.stop_after == 'B0':
                S.flush(final=True)
                return nc
            tilesB = [[(t, TT, 0)] for t in range(0, NP_, TT)]
            s0 = NP_ + HALO
            tilesB += [[(s0 + t, TT, 0)] for t in range(0, OWN, TT)]
            tilesB.append([(s0 - QH, QH, 0), (s0 + OWN, QH, QH)])
            with ExitStack() as e3:
                alloc_common(e3, '_b')
                with nc.sbuf_tensor("oT", [128, 9, TT], BF16) as oT, nc.sbuf_tensor("vmask", [128, TT], F32) as vmask, \
                        nc.sbuf_tensor("gT", [128, 8, TT], BF16) as gT:
                    self.phaseB1(tilesB, dict(oT=oT, vmask=vmask, gT=gT))
                    S.flush()
                if self.stop_after == 'B1':
                    S.flush(final=True)
                    return nc
                tilesC = [(t, 0, NP_, t) for t in range(0, NP_, TT)]
                tilesC += [(s0 + t, NP_, NL, NP_ + t) for t in range(0, OWN, TT)]
                with nc.sbuf_tensor("gpad", [128, 8, TT + 30], BF16) as gpad, nc.sbuf_tensor("cacc", [128, 8, TT], F32) as cacc:
                    self.phaseC(tilesC, dict(gpad=gpad, cacc=cacc))
                    S.flush(final=True)
        return nc


def _t5_bucket(rel):
    half = 16
    max_exact = 8
    ret = np.where(rel > 0, half, 0)
    n = np.abs(rel)
    nf = np.maximum(n, 1).astype(np.float32)
    large = max_exact + (np.log(nf / np.float32(max_exact)) / np.float32(math.log(1024 / max_exact))
                         * np.float32(half - max_exact)).astype(np.int32)
    large = np.minimum(large, half - 1)
    return ret + np.where(n < max_exact, n, large)


def _bias_tables(rel_bias):
    kk = np.arange(128)[:, None]
    cc = np.arange(256)[None, :]
    rel = kk - cc + 64
    ok = np.abs(rel) <= 64
    b0 = np.full((128, 18, 256), -30000.0, np.float32)
    for g, dil in enumerate(DILS):
        bk = _t5_bucket((rel * dil).astype(np.int32))
        for h in range(6):
            gh = g * 6 + h
            b0[:, gh, :] = np.where(ok, rel_bias[bk, gh], np.float32(-30000.0))
    b1 = np.full((128, 18, 128), -30000.0, np.float32)
    b1[0:64] = b0[64:128, :, 128:256]
    return np.ascontiguousarray(b0.reshape(128, 18 * 256)), np.ascontiguousarray(b1.reshape(128, 18 * 128))


def _cols(v, n):
    return np.ascontiguousarray(np.asarray(v, np.float32).reshape(n, 128).T)


_PROG = {}


def kernel(**inp):
    x_prompt = np.asarray(inp['x_prompt'], np.float32)
    x_sample = np.asarray(inp['x_sample'], np.float32)
    vecs = np.zeros((128, NV), np.float32)
    for l in range(4):
        vecs[:, VOFF[('nf1', l)]:VOFF[('nf1', l)] + 8] = _cols(inp['norm_ffn1'][l], 8)
        vecs[:, VOFF[('nmx', l)]:VOFF[('nmx', l)] + 8] = _cols(inp['norm_mix'][l], 8)
        vecs[:, VOFF[('nf2', l)]:VOFF[('nf2', l)] + 8] = _cols(inp['norm_ffn2'][l], 8)
    vecs[:, VOFF['nfin']:VOFF['nfin'] + 8] = _cols(inp['norm_final'], 8)
    for j in range(2):
        vecs[:, VOFF[('agv', j)]:VOFF[('agv', j)] + 24] = _cols(inp['a_g_v'][j], 24)
    vecs[:, VOFF['cb1']:VOFF['cb1'] + 16] = _cols(inp['c_b_pw1'][0], 16)
    wdw = np.asarray(inp['c_w_dw'][0], np.float32)
    vecs[:, VOFF['cdw']:VOFF['cdw'] + 248] = wdw.reshape(31, 8, 128).transpose(2, 0, 1).reshape(128, 248)
    vecs[:, VOFF['cbdw']:VOFF['cbdw'] + 8] = _cols(inp['c_b_dw'][0], 8)
    vecs[:, VOFF['cgn']:VOFF['cgn'] + 8] = _cols(inp['c_g_norm'][0], 8)
    vecs[:, VOFF['cb2']:VOFF['cb2'] + 8] = _cols(inp['c_b_pw2'][0], 8)
    b0, b1 = _bias_tables(np.asarray(inp['rel_bias'], np.float32))
    ident = np.eye(128, dtype=np.float32)
    m4 = np.array([[1, 0, 1, 0], [1, 0, 0, 1], [0, 1, 1, 0], [0, 1, 0, 1]], np.float32)
    shared = {k: np.ascontiguousarray(np.asarray(inp[k], np.float32)) for k in (
        "ffn1_w_gate", "ffn1_w_up", "ffn1_w_down", "ffn2_w_gate", "ffn2_w_up", "ffn2_w_down",
        "a_w_in", "a_w_out", "a_w_spatial", "a_b_spatial", "a_g_v", "b_w_qkv", "b_w_out", "c_w_pw1", "c_w_pw2")}
    in_maps = []
    for core in range(NCORES):
        s, q = core // 4, core % 4
        g0 = q * OWN - HALO
        xl = np.zeros((NL, D), np.float32)
        xl[0:NP_] = x_prompt[core]
        valid = np.zeros((NL,), np.float32)
        valid[0:NP_] = 1.0
        a, b = max(g0, 0), min(g0 + NS, x_sample.shape[1])
        xl[NP_ + (a - g0):NP_ + (b - g0)] = x_sample[s, a:b]
        valid[NP_ + (a - g0):NP_ + (b - g0)] = 1.0
        m = dict(shared)
        m.update(xl=xl, vecs=vecs, validcol=np.ascontiguousarray(valid.reshape(NL // 128, 128).T),
                 validrow=valid.reshape(1, NL).copy(), ident=ident, braw0=b0, braw1=b1, m4=m4)
        in_maps.append(m)
    if 'nc' not in _PROG:
        _PROG['nc'] = Builder().build()
    res = run_bass_kernel_spmd(_PROG['nc'], in_maps, core_ids=list(range(NCORES)))
    y_prompt = np.zeros_like(x_prompt)
    y_sample = np.zeros_like(x_sample)
    for core in range(NCORES):
        s, q = core // 4, core % 4
        y = np.asarray(res.results[core]["y"])
        y_prompt[core] = y[0:NP_]
        y_sample[s, q * OWN:(q + 1) * OWN] = y[NP_:NP_ + OWN]
    return (y_prompt, y_sample)
```

```python
import math
from contextlib import ExitStack

import numpy as np
import concourse.bass as bass
import concourse.mybir as mybir
from concourse.bass_utils import run_bass_kernel_spmd

F32 = mybir.dt.float32
BF16 = mybir.dt.bfloat16
AF = mybir.ActivationFunctionType
ALU = mybir.AluOpType

D = 1024
DFF = 2816
KC = 8
NFC = 22
AH = 3072
NP_ = 2048
HALO = 1152
OWN = 4096
NS = OWN + 2 * HALO
NL = NP_ + NS
QH = 16
EPS = 1e-6
NCORES = 8
TT = 512
NSLOT = 5
SLOT_ELEMS = 4096
DILS = (1, 4, 16)
VW = 1152 + 128
B0_SKEW = False

ENGS = ('pe', 'act', 'dve', 'pool', 'sp')
COMPUTE = ('pe', 'act', 'dve', 'pool')
BLOCKNAME = {'pe': 'tensor', 'act': 'scalar', 'dve': 'vector', 'pool': 'gpsimd', 'sp': 'sync'}
KDMA = 8


def _vec_layout():
    off = {}
    n = 0
    for l in range(4):
        off[('nf1', l)] = n; n += 8
        off[('nmx', l)] = n; n += 8
        off[('nf2', l)] = n; n += 8
    off['nfin'] = n; n += 8
    for j in range(2):
        off[('agv', j)] = n; n += 24
    off['cb1'] = n; n += 16
    off['cdw'] = n; n += 31 * 8
    off['cbdw'] = n; n += 8
    off['cgn'] = n; n += 8
    off['cb2'] = n; n += 8
    return off, n


VOFF, NV = _vec_layout()


class Op:
    __slots__ = ('eng', 'fn', 'deps', 'signal', 'sig', 'cover', 'dma', 'sem', 'semval')

    def __init__(self, eng, fn, dma):
        self.eng = eng
        self.fn = fn
        self.deps = []
        self.signal = False
        self.sig = None
        self.cover = None
        self.dma = dma
        self.sem = None
        self.semval = 0


class Sched:
    def __init__(self, nc, es):
        self.nc = nc
        self.csem = {e: es.enter_context(nc.semaphore("c_" + e)) for e in COMPUTE}
        self.dsem = {q: [es.enter_context(nc.semaphore("d_%s%d" % (q, i))) for i in range(KDMA)]
                     for q in ('sp', 'pool')}
        self.dcount = {'sp': 0, 'pool': 0}
        self.sigcount = {e: 0 for e in COMPUTE}
        self.pending = {e: [] for e in ENGS}
        self.lastw = {}
        self.readers = {}
        self.seen = {e: {} for e in ENGS}
        self.prefix = []
        self.nops = 0

    def op(self, eng, fn, reads=(), writes=(), dma=False):
        o = Op(eng, fn, dma)
        self.nops += 1
        cand = []
        for k in reads:
            w = self.lastw.get(k)
            if w is not None:
                cand.append((w, 0))
        for k in writes:
            w = self.lastw.get(k)
            if w is not None:
                cand.append((w, 1))
            rd = self.readers.get(k)
            if rd:
                for r in rd.values():
                    if isinstance(r, list):
                        for r2 in r:
                            cand.append((r2, 2))
                    else:
                        cand.append((r, 2))
        seen_ids = set()
        for p, kind in cand:
            if p is o or id(p) in seen_ids:
                continue
            if p.eng == eng and (not p.dma) and (not dma):
                if eng == 'pe':
                    continue
                if kind != 0:
                    continue
            seen_ids.add(id(p))
            o.deps.append(p)
            if not p.dma:
                p.signal = True
        for k in reads:
            rd = self.readers.setdefault(k, {})
            if dma:
                rd.setdefault('dma_' + eng, []).append(o)
            else:
                rd[eng] = o
        for k in writes:
            self.lastw[k] = o
            self.readers[k] = {}
        if dma:
            n = self.dcount[eng]
            self.dcount[eng] = n + 1
            o.sem = self.dsem[eng][n % KDMA]
            o.semval = 16 * (n // KDMA + 1)
        self.pending[eng].append(o)
        return o

    def _all_final(self):
        fin = []
        for e in COMPUTE:
            if self.sigcount[e] > 0:
                fin.append((('c', e), self.csem[e], self.sigcount[e]))
        for q in ('sp', 'pool'):
            n = self.dcount[q]
            for i in range(KDMA):
                cnt = (n - i + KDMA - 1) // KDMA if n > i else 0
                if cnt > 0:
                    fin.append((('d', q, i), self.dsem[q][i], 16 * cnt))
        return fin

    def flush(self, final=False, skip_pool_prefix=False):
        nc = self.nc
        for e in COMPUTE:
            ops = [o for o in self.pending[e] if not o.dma]
            if ops:
                ops[-1].signal = True
            cnt = self.sigcount[e]
            for o in ops:
                if o.signal:
                    cnt += 1
                    o.sig = cnt
            self.sigcount[e] = cnt
            nxt = None
            for o in reversed(ops):
                if o.signal:
                    nxt = o.sig
                o.cover = nxt
        prefix = self.prefix
        fin = self._all_final() if final else None
        dkey = {}
        for q in ('sp', 'pool'):
            for i in range(KDMA):
                dkey[id(self.dsem[q][i])] = ('d', q, i)
        with nc.Block() as block:
            for e in ENGS:
                ops = self.pending[e]

                def body(eng, e=e, ops=ops):
                    seen = self.seen[e]
                    for key, sem, val in prefix:
                        if seen.get(key, 0) < val:
                            eng.wait_ge(sem, val)
                            seen[key] = val
                    for o in ops:
                        waits = {}
                        for p in o.deps:
                            if p.dma:
                                key, s, v = dkey[id(p.sem)], p.sem, p.semval
                            else:
                                key, s, v = ('c', p.eng), self.csem[p.eng], p.cover
                            if key not in waits or waits[key][1] < v:
                                waits[key] = (s, v)
                        if o.dma and o.semval > 16:
                            key = dkey[id(o.sem)]
                            v = o.semval - 16
                            if key not in waits or waits[key][1] < v:
                                waits[key] = (o.sem, v)
                        for key, (s, v) in waits.items():
                            if seen.get(key, 0) < v:
                                eng.wait_ge(s, v)
                                seen[key] = v
                        inst = o.fn(eng)
                        if o.dma:
                            inst.then_inc(o.sem, 16)
                        elif o.signal:
                            inst.then_inc(self.csem[e], 1)
                    if fin is not None and e == 'sp':
                        for key, sem, val in fin:
                            if seen.get(key, 0) < val:
                                eng.wait_ge(sem, val)
                                seen[key] = val

                getattr(block, BLOCKNAME[e])(body)
        self.pending = {e: [] for e in ENGS}
        self.prefix = self._all_final()
        if skip_pool_prefix:
            self.prefix = [p for p in self.prefix if not (p[0][0] == 'd' and p[0][1] == 'pool')]


def dap(handle, offset, dims):
    return bass.AP(handle, offset, [[int(s), int(n)] for s, n in dims])


class Builder:
    def __init__(self, stop_after=None, dbg=False):
        self.stop_after = stop_after
        self.dbg = dbg
        self.nc = bass.Bass("TRN2", target_bir_lowering=False)
        self.es = ExitStack()
        self.wslot_n = 0
        self.mm_n = 0
        self.out_n = 0
        self.tmp_n = 0
        self.norm_pre = None
        self.convq = {}
        self.conv_rest = []

    def declare(self):
        nc = self.nc

        def inp(name, shape):
            return nc.dram_tensor(name, list(shape), F32, kind="ExternalInput")

        self.xl = inp("xl", [NL, D])
        self.w = {}
        for nm, shp in (
            ("ffn1_w_gate", [4, D, DFF]), ("ffn1_w_up", [4, D, DFF]), ("ffn1_w_down", [4, DFF, D]),
            ("ffn2_w_gate", [4, D, DFF]), ("ffn2_w_up", [4, D, DFF]), ("ffn2_w_down", [4, DFF, D]),
            ("a_w_in", [2, D, 2 * AH]), ("a_w_out", [2, AH, D]),
            ("a_w_spatial", [2, 8, 128, 128]), ("a_b_spatial", [2, 8, 128]), ("a_g_v", [2, AH]),
            ("b_w_qkv", [1, D, 3456]), ("b_w_out", [1, 1152, D]),
            ("c_w_pw1", [1, D, 2 * D]), ("c_w_pw2", [1, D, D]),
        ):
            self.w[nm] = inp(nm, shp)
        self.w_agv = self.w['a_g_v']
        self.vecs_d = inp("vecs", [128, NV])
        self.validcol_d = inp("validcol", [128, NL // 128])
        self.validrow_d = inp("validrow", [1, NL])
        self.ident_d = inp("ident", [128, 128])
        self.braw0_d = inp("braw0", [128, 18 * 256])
        self.braw1_d = inp("braw1", [128, 18 * 128])
        self.m4_d = inp("m4", [4, 4])
        self.y = nc.dram_tensor("y", [NP_ + OWN, D], F32, kind="ExternalOutput")

        def scratch(name, shape, dt):
            kind = "ExternalOutput" if (self.dbg and name in self.dbg) else "Internal"
            return nc.dram_tensor(name, list(shape), dt, kind=kind)

        self.XS1 = scratch("XS1", [D, NL], F32)
        self.XS2 = scratch("XS2", [D, NL], F32)
        self.GT = scratch("GT", [D, NL], BF16)
        self.QT = scratch("QT", [1152, NL], BF16)
        self.KT = scratch("KT", [1152, NL], BF16)
        self.VV = scratch("VV", [NL, VW], BF16)
        self.OT = scratch("OT", [1152, NL], BF16)
        self.BL = scratch("BL", [2, 4, AH], BF16)
        self.BR = scratch("BR", [2, 4, 4096], BF16)
        self.streams = {}
        for i in range(8):
            self.streams['f1_%d' % i] = (11, 4096)
            self.streams['f2_%d' % i] = (8, 2816)
        for j in range(2):
            self.streams['ain_%d' % j] = (12, 4096)
            self.streams['aout_%d' % j] = (8, 3072)
        self.streams['qkv'] = (9, 3072)
        self.streams['wo'] = (8, 1152)
        self.streams['pw1'] = (4, 4096)
        self.streams['pw2'] = (8, 1024)
        self.streams['dwd'] = (8, 3968)
        self.wbf = {}
        for nm, (npc, el) in self.streams.items():
            self.wbf[nm] = nc.dram_tensor("wbf_" + nm, [npc, 128, el], BF16, kind="Internal")

    def conv_dma(self, dst_h, dst_off, dst_dims, src_h, src_off, src_dims, key=None):
        out_ap = dap(dst_h, dst_off, dst_dims)
        in_ap = dap(src_h, src_off, src_dims)
        self.convq.setdefault(key[0], []).append((out_ap, in_ap, key))

    def emit_conv(self, streams=None, n=None):
        S = self.S
        todo = []
        if streams is not None:
            for st in streams:
                todo += self.convq.pop(st, [])
        else:
            while n > 0 and self.conv_rest:
                st = self.conv_rest[0]
                lst = self.convq.get(st, [])
                while lst and n > 0:
                    todo.append(lst.pop(0))
                    n -= 1
                if not lst:
                    self.conv_rest.pop(0)
                    self.convq.pop(st, None)
        for out_ap, in_ap, key in todo:
            S.op('pool', lambda g, out_ap=out_ap, in_ap=in_ap: g.dma_start(out=out_ap, in_=in_ap), writes=[key], dma=True)

    def phase0(self):
        for i in range(8):
            l = i // 2
            pre = "ffn1" if i % 2 == 0 else "ffn2"
            wg, wu, wd = self.w[pre + "_w_gate"], self.w[pre + "_w_up"], self.w[pre + "_w_down"]
            dst = self.wbf['f1_%d' % i]
            for fb in range(11):
                for gu, src in enumerate((wg, wu)):
                    self.conv_dma(dst, fb * 128 * 4096 + gu * 2048, [(4096, 128), (256, 8), (1, 256)],
                                  src, l * D * DFF + fb * 256, [(DFF, 128), (128 * DFF, 8), (1, 256)], key=('f1_%d' % i, fb, gu))
            dst = self.wbf['f2_%d' % i]
            for dc in range(8):
                self.conv_dma(dst, dc * 128 * 2816, [(2816, 128), (128, 22), (1, 128)],
                              wd, l * DFF * D + dc * 128, [(D, 128), (128 * D, 22), (1, 128)], key=('f2_%d' % i, dc, 0))
        for j in range(2):
            dst = self.wbf['ain_%d' % j]
            for pc in list(range(6, 12)) + list(range(6)):
                self.conv_dma(dst, pc * 128 * 4096, [(4096, 128), (512, 8), (1, 512)],
                              self.w['a_w_in'], j * D * 2 * AH + pc * 512,
                              [(2 * AH, 128), (128 * 2 * AH, 8), (1, 512)], key=('ain_%d' % j, pc, 0))
            dst = self.wbf['aout_%d' % j]
            for dc in range(8):
                self.conv_dma(dst, dc * 128 * 3072, [(3072, 128), (128, 24), (1, 128)],
                              self.w['a_w_out'], j * AH * D + dc * 128, [(D, 128), (128 * D, 24), (1, 128)], key=('aout_%d' % j, dc, 0))
        dst = self.wbf['qkv']
        for pc in range(9):
            self.conv_dma(dst, pc * 128 * 3072, [(3072, 128), (384, 8), (1, 384)],
                          self.w['b_w_qkv'], pc * 384, [(3456, 128), (128 * 3456, 8), (1, 384)], key=('qkv', pc, 0))
        dst = self.wbf['wo']
        for dc in range(8):
            self.conv_dma(dst, dc * 128 * 1152, [(1152, 128), (128, 9), (1, 128)],
                          self.w['b_w_out'], dc * 128, [(D, 128), (128 * D, 9), (1, 128)], key=('wo', dc, 0))
        dst = self.wbf['pw1']
        for fb in range(4):
            for ag in range(2):
                self.conv_dma(dst, fb * 128 * 4096 + ag * 2048, [(4096, 128), (256, 8), (1, 256)],
                              self.w['c_w_pw1'], ag * D + fb * 256, [(2 * D, 128), (128 * 2 * D, 8), (1, 256)], key=('pw1', fb, ag))
        dst = self.wbf['pw2']
        for dc in range(8):
            self.conv_dma(dst, dc * 128 * 1024, [(1024, 128), (128, 8), (1, 128)],
                          self.w['c_w_pw2'], dc * 128, [(D, 128), (128 * D, 8), (1, 128)], key=('pw2', dc, 0))

    def load_consts(self):
        S, nc = self.S, self.nc
        es = self.es
        self.vecs = es.enter_context(nc.sbuf_tensor("vecs_sb", [128, NV], F32))
        self.validcol = es.enter_context(nc.sbuf_tensor("validcol_sb", [128, NL // 128], F32))
        self.ident = es.enter_context(nc.sbuf_tensor("ident_sb", [128, 128], F32))
        self.ones_mean = es.enter_context(nc.sbuf_tensor("ones_mean", [128, 128], BF16))
        self.ones_bf = es.enter_context(nc.sbuf_tensor("ones_bf", [128, 128], BF16))
        self.wst32 = es.enter_context(nc.sbuf_tensor("wst32", [128, 2, 8, 128], F32))
        self.ps = es.enter_context(nc.psum_tensor("ps", [128, 8, 512], F32))
        S.op('sp', lambda q: q.dma_start(out=self.vecs[:], in_=self.vecs_d.ap()), writes=['vecs'], dma=True)
        S.op('sp', lambda q: q.dma_start(out=self.validcol[:], in_=self.validcol_d.ap()), writes=['validcol'],
             dma=True)
        S.op('sp', lambda q: q.dma_start(out=self.ident[:], in_=self.ident_d.ap()), writes=['ident'], dma=True)
        S.op('dve', lambda v: v.memset(self.ones_mean[:], 1.0 / 1024.0), writes=['ones_mean'])
        S.op('dve', lambda v: v.memset(self.ones_bf[:], 1.0), writes=['ones_bf'])

    def load_wst(self, wsin):
        S = self.S
        ps = self.ps
        for j in range(2):
            for g in range(8):
                k = j * 8 + g
                src = dap(self.w['a_w_spatial'], k * 128 * 128, [(128, 128), (1, 128)])
                buf = wsin[:, k % 2, :]
                S.op('sp', lambda q, buf=buf, src=src: q.dma_start(out=buf, in_=src),
                     writes=[('wsin', k % 2)], dma=True)
                bank = 6 + (k % 2)
                S.op('pe', lambda t, buf=buf, bank=bank: t.transpose(ps[:, bank, 0:128], buf, self.ident[:]),
                     reads=[('wsin', k % 2), 'ident'], writes=[('ps', bank)])
                S.op('act', lambda a, bank=bank, j=j, g=g: a.activation(out=self.wst32[:, j, g, :],
                                                                           in_=ps[:, bank, 0:128], func=AF.Copy),
                     reads=[('ps', bank)], writes=['wst32'])

    def build_bias_rows(self):
        S, nc = self.S, self.nc
        with nc.sbuf_tensor("bb32", [4, AH], F32) as t32, nc.sbuf_tensor("bbhi", [4, AH], BF16) as thi, \
                nc.sbuf_tensor("bblo32", [4, AH], F32) as tlo32, nc.sbuf_tensor("bblo", [4, AH], BF16) as tlo, \
                nc.sbuf_tensor("bbfin", [4, AH], BF16) as tfin, nc.sbuf_tensor("m4sb", [4, 4], F32) as m4:
            S.op('sp', lambda q: q.dma_start(out=m4[:], in_=self.m4_d.ap()), writes=['m4'], dma=True)
            for j in range(2):
                for kind in range(2):
                    n = AH if kind == 0 else 1024
                    if kind == 0:
                        src = dap(self.w_agv, j * AH, [(0, 4), (1, AH)])
                    else:
                        src = dap(self.w['a_b_spatial'], j * 1024, [(0, 4), (1, 1024)])
                    S.op('sp', lambda q, src=src, n=n: q.dma_start(out=t32[:, 0:n], in_=src), writes=['bb32'], dma=True)
                    if kind == 0:
                        S.op('dve', lambda v, n=n: v.reciprocal(out=t32[:, 0:n], in_=t32[:, 0:n]), reads=['bb32'], writes=['bb32'])
                    S.op('dve', lambda v, n=n: v.tensor_copy(out=thi[:, 0:n], in_=t32[:, 0:n]), reads=['bb32'], writes=['bbhi'])
                    S.op('dve', lambda v, n=n: v.tensor_tensor(out=tlo32[:, 0:n], in0=t32[:, 0:n], in1=thi[:, 0:n], op=ALU.subtract),
                         reads=['bb32', 'bbhi'], writes=['bblo32'])
                    S.op('dve', lambda v, n=n: v.tensor_copy(out=tlo[:, 0:n], in_=tlo32[:, 0:n]), reads=['bblo32'], writes=['bblo'])
                    mc = 0 if kind == 0 else 2
                    S.op('dve', lambda v, n=n, mc=mc: v.tensor_scalar(out=tfin[:, 0:n], in0=thi[:, 0:n], scalar1=m4[:, mc:mc + 1],
                                                                      scalar2=None, op0=ALU.mult),
                         reads=['bbhi', 'm4'], writes=['bbfin'])
                    S.op('dve', lambda v, n=n, mc=mc: v.scalar_tensor_tensor(out=tfin[:, 0:n], in0=tlo[:, 0:n], scalar=m4[:, mc + 1:mc + 2],
                                                                             in1=tfin[:, 0:n], op0=ALU.mult, op1=ALU.add),
                         reads=['bblo', 'bbfin', 'm4'], writes=['bbfin'])
                    if kind == 0:
                        dst = dap(self.BL, j * 4 * AH, [(AH, 4), (1, AH)])
                        S.op('sp', lambda q, dst=dst: q.dma_start(out=dst, in_=tfin[:, 0:AH]), reads=['bbfin'], writes=[('BL', j)], dma=True)
                    else:
                        for sb in range(4):
                            dst = dap(self.BR, j * 4 * 4096 + sb * 128, [(4096, 4), (512, 8), (1, 128)])
                            S.op('sp', lambda q, dst=dst: q.dma_start(out=dst, in_=tfin[:, 0:1024].rearrange("p (a b) -> p a b", b=128)),
                                 reads=['bbfin'], writes=[('BR', j, sb)], dma=True)

    def wload(self, stream, piece):
        S = self.S
        npc, el = self.streams[stream]
        slot = self.wslot_n % NSLOT
        self.wslot_n += 1
        src = dap(self.wbf[stream], piece * 128 * el, [(el, 128), (1, el)])
        dst = self.wring[:, slot, 0:el]
        rk = [(stream, piece, 0)]
        if stream.startswith('f1_') or stream == 'pw1':
            rk.append((stream, piece, 1))
        S.op('sp', lambda q: q.dma_start(out=dst, in_=src), reads=rk, writes=[('w', slot)], dma=True)
        return slot

    def wview(self, slot, dims):
        base = self.wring[:, slot, :]
        t, off, pst = base.tensor, base.offset, base.ap[0][0]
        strides = []
        s = 1
        for d in reversed(dims):
            strides.append(s)
            s *= d
        strides = list(reversed(strides))

        def view(idx, last):
            o = off
            for i, st in zip(idx, strides[:-1]):
                o = o + i * st
            o = o + last[0]
            return bass.AP(t, o, [[pst, 128], [1, last[1]]])

        return view

    def mmbank(self):
        b = self.mm_n % 4
        self.mm_n += 1
        return b

    def outbank(self):
        b = 4 + self.out_n % 2
        self.out_n += 1
        return b

    def tmpslot(self):
        b = self.tmp_n % 2
        self.tmp_n += 1
        return b

    def rmsnorm(self, T, src, src_keys, gcol, dst_fn, dst_keys, dst_eng='dve', post=None):
        S, ps = self.S, self.ps
        hT = self.hT
        pre = (src_keys[0] == ('xT', 0)) and (self.norm_pre == T)
        self.norm_pre = None
        if not pre:
            for c in range(KC):
                if c % 2 == 0:
                    S.op('act', lambda a, c=c: a.activation(out=hT[:, c, :T], in_=src(c), func=AF.Square),
                         reads=[src_keys[c]], writes=[('hT', c)])
                else:
                    S.op('dve', lambda v, c=c: v.tensor_tensor(out=hT[:, c, :T], in0=src(c), in1=src(c), op=ALU.mult),
                         reads=[src_keys[c]], writes=[('hT', c)])
            for c in range(KC):
                S.op('pe', lambda t, c=c: t.matmul(ps[:, 6, :T], lhsT=self.ones_mean[:], rhs=hT[:, c, :T],
                                                     start=(c == 0), stop=(c == KC - 1)),
                     reads=[('hT', c), 'ones_mean'], writes=[('ps', 6)])
        S.op('act', lambda a: a.activation(out=self.rt[:, 0, :T], in_=ps[:, 6, :T], func=AF.Ln, bias=self.epsc[:, 0:1]),
             reads=[('ps', 6), 'epsc'], writes=['rt0'])
        S.op('act', lambda a: a.activation(out=self.rt[:, 1, :T], in_=self.rt[:, 0, :T], func=AF.Exp, scale=-0.5),
             reads=['rt0'], writes=['rstd'])
        for c in range(KC):
            S.op('dve', lambda v, c=c: v.scalar_tensor_tensor(out=dst_fn(c), in0=src(c), scalar=self.vecs[:, gcol + c:gcol + c + 1],
                                                              in1=self.rt[:, 1, :T], op0=ALU.mult, op1=ALU.mult),
                 reads=[src_keys[c], 'rstd', 'vecs'], writes=[dst_keys[c]])

    def pre_sq(self, dc, T):
        S = self.S
        S.op('act', lambda a: a.activation(out=self.hT[:, dc, :T], in_=self.xT[:, dc, :T], func=AF.Square),
             reads=[('xT', dc)], writes=[('hT', dc)])

    def pre_mm(self, dc, T):
        S, ps = self.S, self.ps
        S.op('pe', lambda t: t.matmul(ps[:, 6, :T], lhsT=self.ones_mean[:], rhs=self.hT[:, dc, :T],
                                      start=(dc == 0), stop=(dc == KC - 1)),
             reads=[('hT', dc), 'ones_mean'], writes=[('ps', 6)])
        if dc == KC - 1:
            self.norm_pre = T

    def xT_src(self, T):
        return lambda c: self.xT[:, c, :T]

    def ffn(self, T, i):
        S, ps = self.S, self.ps
        l = i // 2
        gcol = VOFF[('nf1', l)] if i % 2 == 0 else VOFF[('nf2', l)]
        hT, act, xT = self.hT, self.big, self.xT
        xk = [('xT', c) for c in range(KC)]
        hk = [('hT', c) for c in range(KC)]
        self.rmsnorm(T, self.xT_src(T), xk, gcol, lambda c: hT[:, c, :T], hk)
        for fb in range(11):
            slot = self.wload('f1_%d' % i, fb)
            wv = self.wview(slot, [2, 8, 256])
            fbanks = [(self.mmbank(), self.mmbank()) for _ in range(2)]
            if fb == 0:
                for kc in range(KC):
                    for j in range(2):
                        for gu in range(2):
                            if (j, gu) == (1, 1):
                                continue
                            bank = fbanks[j][gu]
                            S.op('pe', lambda t, gu=gu, bank=bank, kc=kc, j=j, wv=wv: t.matmul(
                                ps[:, bank, :T], lhsT=wv((gu, kc), (j * 128, 128)), rhs=hT[:, kc, :T],
                                start=(kc == 0), stop=(kc == KC - 1)),
                                 reads=[('w', slot), ('hT', kc)], writes=[('ps', bank)])
            for j in range(2):
                fc = fb * 2 + j
                bg, bu = fbanks[j]
                if fb != 0 or j == 1:
                    for gu, bank in ((0, bg), (1, bu)):
                        if fb == 0 and gu == 0:
                            continue
                        for kc in range(KC):
                            S.op('pe', lambda t, gu=gu, bank=bank, kc=kc, j=j, wv=wv: t.matmul(
                                ps[:, bank, :T], lhsT=wv((gu, kc), (j * 128, 128)), rhs=hT[:, kc, :T],
                                start=(kc == 0), stop=(kc == KC - 1)),
                                 reads=[('w', slot), ('hT', kc)], writes=[('ps', bank)])
                ts = self.tmpslot()
                S.op('act', lambda a, bg=bg, ts=ts: a.activation(out=self.tmpf[:, ts, :T], in_=ps[:, bg, :T], func=AF.Silu),
                     reads=[('ps', bg)], writes=[('tmpf', ts)])
                S.op('dve', lambda v, bu=bu, ts=ts, fc=fc: v.tensor_tensor(out=act[:, fc * 512:fc * 512 + T], in0=self.tmpf[:, ts, :T],
                                                                             in1=ps[:, bu, :T], op=ALU.mult),
                     reads=[('ps', bu), ('tmpf', ts)], writes=[('big', fc)])
        S.op('act', lambda a: a.activation(out=self.vst[:, 38:39], in_=self.epsc[:, 0:1], func=AF.Ln), reads=['epsc'], writes=['dummy_ln'])
        for dc in range(8):
            slot = self.wload('f2_%d' % i, dc)
            wv = self.wview(slot, [22, 128])
            bank = self.outbank()
            for fc in range(NFC):
                S.op('pe', lambda t, fc=fc, bank=bank, wv=wv: t.matmul(
                    ps[:, bank, :T], lhsT=wv((fc,), (0, 128)), rhs=act[:, fc * 512:fc * 512 + T],
                    start=(fc == 0), stop=(fc == NFC - 1)),
                     reads=[('w', slot), ('big', fc)], writes=[('ps', bank)])
            if dc > 0:
                self.pre_mm(dc - 1, T)
            S.op('dve', lambda v, dc=dc, bank=bank: v.scalar_tensor_tensor(out=xT[:, dc, :T], in0=ps[:, bank, :T], scalar=0.5,
                                                                             in1=xT[:, dc, :T], op0=ALU.mult, op1=ALU.add),
                 reads=[('ps', bank), ('xT', dc)], writes=[('xT', dc)])
            self.pre_sq(dc, T)
        self.pre_mm(7, T)

    def gmlp(self, T, j, l):
        S, ps = self.S, self.ps
        hT, xT, big = self.hT, self.xT, self.big
        nsb = T // 128
        xk = [('xT', c) for c in range(KC)]
        hk = [('hT', c) for c in range(KC)]
        self.rmsnorm(T, self.xT_src(T), xk, VOFF[('nmx', l)], lambda c: hT[:, c, :T], hk)
        blsrc = dap(self.BL, j * 4 * AH, [(AH, 4), (1, AH)])
        brsrc = dap(self.BR, j * 4 * 4096, [(4096, 4), (1, 4096)])
        S.op('pool', lambda g: g.dma_start(out=self.bl[0:4, :], in_=blsrc), writes=['bl'], dma=True)
        S.op('pool', lambda g: g.dma_start(out=self.br[0:4, :], in_=brsrc), writes=['br'], dma=True)

        def uT(fc):
            return big[:, fc * 512:fc * 512 + T]

        VB0 = 24 * 512

        def vb(sb, f0, n):
            o = VB0 + sb * AH + f0
            return big[:, o:o + n]

        for pc in range(6):
            slot = self.wload('ain_%d' % j, 6 + pc)
            wv = self.wview(slot, [8, 512])
            vbanks = [self.mmbank() for _ in range(nsb)]
            nwave = min(nsb, 3) if pc == 0 else 0
            if pc == 0:
                for kc in range(KC):
                    for sb in range(nwave):
                        bank = vbanks[sb]
                        S.op('pe', lambda t, bank=bank, kc=kc, sb=sb, wv=wv: t.matmul(
                            ps[:, bank, :], lhsT=hT[:, kc, sb * 128:(sb + 1) * 128], rhs=wv((kc,), (0, 512)),
                            start=(kc == 0), stop=(kc == KC - 1)),
                             reads=[('w', slot), ('hT', kc)], writes=[('ps', bank)])
            for sb in range(nsb):
                bank = vbanks[sb]
                if sb >= nwave:
                    for kc in range(KC):
                        S.op('pe', lambda t, bank=bank, kc=kc, sb=sb, wv=wv: t.matmul(
                            ps[:, bank, :], lhsT=hT[:, kc, sb * 128:(sb + 1) * 128], rhs=wv((kc,), (0, 512)),
                            start=(kc == 0), stop=(kc == KC - 1)),
                             reads=[('w', slot), ('hT', kc)], writes=[('ps', bank)])
                S.op('act', lambda a, bank=bank, sb=sb, pc=pc: a.activation(out=vb(sb, pc * 512, 512), in_=ps[:, bank, :],
                                                                             func=AF.Gelu_apprx_tanh),
                     reads=[('ps', bank)], writes=[('vb', sb)])
                S.op('act', lambda a, sb=sb, pc=pc: a.activation(out=self.junk[:], in_=vb(sb, pc * 512, 512), func=AF.Square,
                                                                  accum_out=self.vst[:, sb * 6 + pc:sb * 6 + pc + 1]),
                     reads=[('vb', sb)], writes=['junk', ('vssq', sb)])
        S.op('act', lambda a: a.activation(out=self.vst[:, 39:40], in_=self.vst[:, 6 * nsb - 1:6 * nsb], func=AF.Copy),
             reads=[('vssq', sb) for sb in range(nsb)], writes=['vfence'])
        S.op('dve', lambda v: v.tensor_reduce(out=self.vst[:, 24:24 + nsb], in_=self.vst[:, 0:6 * nsb].rearrange("p (a b) -> p a b", b=6),
                                              axis=mybir.AxisListType.X, op=ALU.add),
             reads=[('vssq', sb) for sb in range(nsb)] + ['vfence'], writes=['vsum'])
        S.op('act', lambda a: a.activation(out=self.vst[:, 28:28 + nsb], in_=self.vst[:, 24:24 + nsb], func=AF.Ln,
                                           bias=self.epsc[:, 0:1], scale=1.0 / AH),
             reads=['vsum', 'epsc'], writes=['vsq'])
        S.op('act', lambda a: a.activation(out=self.vst[:, 32:32 + nsb], in_=self.vst[:, 28:28 + nsb], func=AF.Exp, scale=-0.5),
             reads=['vsq'], writes=['vrstd'])
        for sb in range(nsb):
            S.op('dve', lambda v, sb=sb: v.tensor_scalar(out=self.wsts[:, sb, :], in0=self.wst32[:, j, :, :].rearrange("p a b -> p (a b)"),
                                                           scalar1=self.vst[:, 32 + sb:33 + sb], scalar2=None, op0=ALU.mult),
                 reads=['vrstd', 'wst32'], writes=[('wsts', sb)])
        for pc in range(6):
            slot = self.wload('ain_%d' % j, pc)
            wv = self.wview(slot, [8, 512])
            for jj in range(4):
                fc = pc * 4 + jj
                bank = self.mmbank()
                for kc in range(KC):
                    S.op('pe', lambda t, bank=bank, kc=kc, jj=jj, wv=wv: t.matmul(
                        ps[:, bank, :T], lhsT=wv((kc,), (jj * 128, 128)), rhs=hT[:, kc, :T],
                        start=(kc == 0), stop=(kc == KC - 1)),
                         reads=[('w', slot), ('hT', kc)], writes=[('ps', bank)])
                S.op('act', lambda a, bank=bank, fc=fc: a.activation(out=uT(fc), in_=ps[:, bank, :T], func=AF.Gelu_apprx_tanh),
                     reads=[('ps', bank)], writes=[('big', fc)])
        gvc = VOFF[('agv', j)]
        for fc in range(24):
            g = fc // 3
            bank = self.mmbank()
            S.op('pe', lambda t, bank=bank, fc=fc, g=g: t.matmul(
                ps[:, bank, :T], lhsT=self.bl[:, fc * 128:(fc + 1) * 128], rhs=self.br[:, g * 512:g * 512 + T],
                start=True, stop=False),
                 reads=['bl', 'br'], writes=[('ps', bank)])
            for sb in range(nsb):
                S.op('pe', lambda t, bank=bank, sb=sb, fc=fc, g=g: t.matmul(
                    ps[:, bank, sb * 128:(sb + 1) * 128], lhsT=vb(sb, fc * 128, 128),
                    rhs=self.wsts[:, sb, g * 128:(g + 1) * 128], start=False, stop=(sb == nsb - 1)),
                     reads=[('vb', sb), ('wsts', sb)], writes=[('ps', bank)])
            S.op('dve', lambda v, bank=bank, fc=fc: v.scalar_tensor_tensor(
                out=uT(fc), in0=ps[:, bank, :T], scalar=self.vecs[:, gvc + fc:gvc + fc + 1], in1=uT(fc),
                op0=ALU.mult, op1=ALU.mult),
                 reads=[('ps', bank), ('big', fc), 'vecs'], writes=[('big', fc)])
        for dc in range(8):
            slot = self.wload('aout_%d' % j, dc)
            wv = self.wview(slot, [24, 128])
            bank = self.outbank()
            for fc in range(24):
                S.op('pe', lambda t, fc=fc, bank=bank, wv=wv: t.matmul(
                    ps[:, bank, :T], lhsT=wv((fc,), (0, 128)), rhs=uT(fc), start=(fc == 0), stop=(fc == 23)),
                     reads=[('w', slot), ('big', fc)], writes=[('ps', bank)])
            if dc > 0:
                self.pre_mm(dc - 1, T)
            S.op('dve', lambda v, dc=dc, bank=bank: v.tensor_tensor(out=xT[:, dc, :T], in0=ps[:, bank, :T], in1=xT[:, dc, :T], op=ALU.add),
                 reads=[('ps', bank), ('xT', dc)], writes=[('xT', dc)])
            self.pre_sq(dc, T)
        self.pre_mm(7, T)

    def fm_io(self, dram, nchunk, sb_tile, pieces, store, keys, q='pool'):
        S = self.S
        for tok0, n, col0 in pieces:
            d = dap(dram, tok0, [(NL, 128), (128 * NL, nchunk), (1, n)])
            s = sb_tile[:, 0:nchunk, col0:col0 + n]
            if store:
                S.op(q, lambda e, d=d, s=s: e.dma_start(out=d, in_=s), reads=keys, dma=True)
            else:
                S.op(q, lambda e, d=d, s=s: e.dma_start(out=s, in_=d), writes=keys, dma=True)

    def phaseA(self, tiles):
        S, nc, ps = self.S, self.nc, self.ps
        xT, hT = self.xT, self.hT
        xk = [('xT', c) for c in range(KC)]
        hk = [('hT', c) for c in range(KC)]
        nrest = sum(len(v) for v in self.convq.values())
        per = (nrest + 5) // 6
        self.phaseA_xload(*tiles[0])
        for g in self.phaseA_tgroups(*tiles[0]):
            g()
        for ti, (tok0, T) in enumerate(tiles):
            nxt = tiles[ti + 1] if ti + 1 < len(tiles) else None
            self.phaseA_tile(tok0, T, nxt)
            self.emit_conv(n=per)
        self.emit_conv(n=100000)

    def phaseA_xload(self, tok0, T):
        S = self.S
        for sb in range(T // 128):
            xb = sb % 4
            src = dap(self.xl, (tok0 + sb * 128) * D, [(D, 128), (1, D)])
            S.op('sp', lambda g, xb=xb, src=src: g.dma_start(out=self.xin[:, xb, :], in_=src),
                 writes=[('xin', xb)], dma=True)

    def phaseA_tgroups(self, tok0, T):
        S, ps, xT = self.S, self.ps, self.xT
        groups = []
        for sb in range(T // 128):
            xb = sb % 4
            for half in range(2):
                def grp(sb=sb, xb=xb, half=half):
                    bank = 6 + half
                    for cc in range(4):
                        c = half * 4 + cc
                        S.op('pe', lambda t, bank=bank, cc=cc, c=c, xb=xb: t.transpose(
                            ps[:, bank, cc * 128:(cc + 1) * 128], self.xin[:, xb, c * 128:(c + 1) * 128], self.ident[:]),
                             reads=[('xin', xb), 'ident'], writes=[('ps', bank)])
                    S.op('act', lambda a, bank=bank, half=half, sb=sb: a.activation(
                        out=xT[:, half * 4:half * 4 + 4, sb * 128:(sb + 1) * 128],
                        in_=ps[:, bank, :].rearrange("p (a b) -> p a b", b=128), func=AF.Copy),
                         reads=[('ps', bank)], writes=[('xT', half * 4 + cc) for cc in range(4)])
                groups.append(grp)
        return groups

    def phaseA_tile(self, tok0, T, nxt=None):
        S, nc, ps = self.S, self.nc, self.ps
        xT, hT = self.xT, self.hT
        xk = [('xT', c) for c in range(KC)]
        hk = [('hT', c) for c in range(KC)]
        if True:
            nsb = T // 128
            self.ffn(T, 0)
            self.gmlp(T, 0, 0)
            self.ffn(T, 1)
            if nxt is not None:
                self.phaseA_xload(*nxt)
            self.ffn(T, 2)
            need_q = not (tok0 >= NP_ and (tok0 + T <= NP_ + HALO - QH or tok0 >= NP_ + HALO + OWN + QH))
            if need_q:
                self.fm_io(self.XS1, 8, xT, [(tok0, T, 0)], True, xk)
            self.rmsnorm(T, self.xT_src(T), xk, VOFF[('nmx', 1)], lambda c: hT[:, c, :T], hk)
            qk = self.qk
            tgroups = self.phaseA_tgroups(*nxt) if nxt is not None else []
            for pc in range(6):
                if pc < 3 and not need_q:
                    continue
                slot = self.wload('qkv', pc)
                wv = self.wview(slot, [8, 384])
                qbanks = [self.mmbank() for _ in range(3)]
                first_wave = (pc == (0 if need_q else 3))
                if first_wave:
                    for kc in range(KC):
                        for jj in range(3):
                            bank = qbanks[jj]
                            S.op('pe', lambda t, bank=bank, kc=kc, jj=jj, wv=wv: t.matmul(
                                ps[:, bank, :T], lhsT=wv((kc,), (jj * 128, 128)), rhs=hT[:, kc, :T],
                                start=(kc == 0), stop=(kc == KC - 1)),
                                 reads=[('w', slot), ('hT', kc)], writes=[('ps', bank)])
                for jj in range(3):
                    c = pc * 3 + jj
                    bank = qbanks[jj]
                    if not first_wave:
                        for kc in range(KC):
                            S.op('pe', lambda t, bank=bank, kc=kc, jj=jj, wv=wv: t.matmul(
                                ps[:, bank, :T], lhsT=wv((kc,), (jj * 128, 128)), rhs=hT[:, kc, :T],
                                start=(kc == 0), stop=(kc == KC - 1)),
                                 reads=[('w', slot), ('hT', kc)], writes=[('ps', bank)])
                    sc = 0.125 if c < 9 else 1.0
                    S.op('act', lambda a, bank=bank, c=c, sc=sc: a.activation(out=qk[:, c, :T], in_=ps[:, bank, :T], func=AF.Copy, scale=sc),
                         reads=[('ps', bank)], writes=[('qk', c)])
                    if tgroups:
                        tgroups.pop(0)()
            while tgroups:
                tgroups.pop(0)()
            if need_q:
                self.fm_io(self.QT, 9, qk, [(tok0, T, 0)], True, [('qk', c) for c in range(9)])
            dK = dap(self.KT, tok0, [(NL, 128), (128 * NL, 9), (1, T)])
            S.op('pool', lambda e, dK=dK, T=T: e.dma_start(out=dK, in_=qk[:, 9:18, 0:T]),
                 reads=[('qk', c) for c in range(9, 18)], dma=True)
            vstg = self.vstg
            for g in range(3):
                slot = self.wload('qkv', 6 + g)
                wv = self.wview(slot, [8, 384])
                for sb in range(nsb):
                    bank = self.mmbank()
                    blk = (tok0 + sb * 128) // 128
                    for kc in range(KC):
                        S.op('pe', lambda t, bank=bank, kc=kc, sb=sb, wv=wv: t.matmul(
                            ps[:, bank, 0:384], lhsT=hT[:, kc, sb * 128:(sb + 1) * 128], rhs=wv((kc,), (0, 384)),
                            start=(kc == 0), stop=(kc == KC - 1)),
                             reads=[('w', slot), ('hT', kc)], writes=[('ps', bank)])
                    S.op('act', lambda a, bank=bank, sb=sb, g=g, blk=blk: a.activation(
                        out=vstg[:, sb, g * 384:(g + 1) * 384], in_=ps[:, bank, 0:384], func=AF.Identity,
                        scale=self.validcol[:, blk:blk + 1]),
                         reads=[('ps', bank), 'validcol'], writes=[('vstg', sb)])
            for sb in range(nsb):
                blk = (tok0 + sb * 128) // 128
                S.op('act', lambda a, sb=sb, blk=blk: a.activation(out=vstg[:, sb, 1152:VW], in_=self.ones_bf[:], func=AF.Identity,
                                                                    scale=self.validcol[:, blk:blk + 1]),
                     reads=['ones_bf', 'validcol'], writes=[('vstg', sb)])
            dV = dap(self.VV, tok0 * VW, [(VW, 128), (128 * VW, nsb), (1, VW)])
            S.op('pool', lambda e, dV=dV, nsb=nsb: e.dma_start(out=dV, in_=vstg[:, 0:nsb, :]),
                 reads=[('vstg', sb) for sb in range(nsb)], dma=True)

    def phaseB0(self, st):
        S, ps = self.S, self.ps
        kbuf, qbuf, E0, E1 = st['kbuf'], st['qbuf'], st['E0'], st['E1']
        numbuf, denbuf, ostage = st['numbuf'], st['denbuf'], st['ostage']
        S.op('sp', lambda q: q.dma_start(out=E0[:], in_=self.braw0_d.ap()), writes=['E0'], dma=True)
        S.op('sp', lambda q: q.dma_start(out=E1[:], in_=self.braw1_d.ap()), writes=['E1'], dma=True)
        S.op('act', lambda a: a.activation(out=E0[:], in_=E0[:], func=AF.Exp), reads=['E0'], writes=['E0'])
        S.op('act', lambda a: a.activation(out=E1[:], in_=E1[:], func=AF.Exp), reads=['E1'], writes=['E1'])
        segs = [(0, NP_, 0, NP_), (NP_, NP_ + NS, NP_ + HALO - QH, NP_ + HALO + OWN + QH)]
        un = 0
        ld = 0
        vn = 0
        vprev = None
        for hp in range(3):
            for (slo, shi, qlo, qhi) in segs:
                NQ = qhi - qlo
                NK = shi - slo
                pend = None
                for g, dil in enumerate(DILS):
                    c = g * 3 + hp
                    kb_i = ld % 2
                    ld += 1
                    dk = dap(self.KT, c * 128 * NL + slo, [(NL, 128), (1, NK)])
                    dq = dap(self.QT, c * 128 * NL + qlo, [(NL, 128), (1, NQ)])
                    S.op('sp', lambda e, dk=dk, kb_i=kb_i, NK=NK: e.dma_start(out=kbuf[:, kb_i, 0:NK], in_=dk),
                         writes=[('kbuf', kb_i)], dma=True)
                    S.op('sp', lambda e, dq=dq, kb_i=kb_i, NQ=NQ: e.dma_start(out=qbuf[:, kb_i, 0:NQ], in_=dq),
                         writes=[('qbuf', kb_i)], dma=True)
                    Lr = NK // dil
                    i_lo = (qlo - slo) // dil
                    nqr = NQ // dil
                    gh0 = g * 6 + 2 * hp
                    for r in range(dil):
                        for a in range(i_lo, i_lo + nqr, 128):
                            nq = min(128, i_lo + nqr - a)
                            qcol0 = (a - i_lo) * dil + r
                            blocks = []
                            if a - 64 < 0:
                                assert a == 0
                                blocks.append((0, min(64, Lr), 1, 0))
                            else:
                                blocks.append((a - 64, min(128, Lr - (a - 64)), 0, 128))
                            nkb = min(nq, Lr - (a + 64))
                            if nkb > 0:
                                blocks.append((a + 64, nkb, 0, 0))
                            par = un % 2
                            un += 1
                            binfo = []
                            for bi, (kstart, nk, tab, cc0) in enumerate(blocks):
                                key_v = (hp, slo, g, r, kstart, nk)
                                if bi == 0 and vprev is not None and vprev[0] == key_v:
                                    vslot, need = vprev[1], False
                                else:
                                    vslot, need = vn % 8, True
                                    vn += 1
                                binfo.append((bi, kstart, nk, tab, cc0, vslot, need))
                                if bi == 1:
                                    vprev = (key_v, vslot)
                            if len(blocks) < 2:
                                vprev = None
                            self._b0_stage1(st, binfo, par, slo, r, dil, c, kb_i, qcol0, nq, gh0)
                            if pend is not None:
                                pend()
                            pend = (lambda binfo=binfo, par=par, nq=nq, g=g, qcol0=qcol0, dil=dil:
                                    self._b0_stage2(st, binfo, par, nq, g, qcol0, dil))
                if pend is not None:
                    pend()
                    pend = None
                S.op('dve', lambda v, NQ=NQ: v.reciprocal(out=denbuf[:, 0:NQ], in_=denbuf[:, 0:NQ]), reads=['den'], writes=['den'])
                for g in range(3):
                    c = g * 3 + hp
                    S.op('dve', lambda v, g=g, NQ=NQ: v.tensor_tensor(out=ostage[:, 0:NQ], in0=numbuf[:, g, 0:NQ], in1=denbuf[:, 0:NQ],
                                                                     op=ALU.mult),
                         reads=[('num', g), 'den'], writes=['ostage'])
                    do = dap(self.OT, c * 128 * NL + qlo, [(NL, 128), (1, NQ)])
                    S.op('pool', lambda e, do=do, NQ=NQ: e.dma_start(out=do, in_=ostage[:, 0:NQ]), reads=['ostage'], dma=True)

    def _b0_stage1(self, st, binfo, par, slo, r, dil, c, kb_i, qcol0, nq, gh0):
        S, ps = self.S, self.ps
        kbuf, qbuf, E0, E1 = st['kbuf'], st['qbuf'], st['E0'], st['E1']
        vblk, es_t, pb_t = st['vblk'], st['es'], st['pb']
        sb0 = 0 if par == 0 else 6
        for (bi, kstart, nk, tab, cc0, vslot, need) in binfo:
            if not need:
                continue
            tok = slo + r + dil * kstart
            dv = dap(self.VV, tok * VW + c * 128, [(dil * VW, nk), (1152 - c * 128, 2), (1, 128)])
            S.op('pool', lambda e, dv=dv, vslot=vslot, nk=nk: e.dma_start(out=vblk[0:nk, vslot, :, :], in_=dv),
                 writes=[('vblk', vslot)], dma=True)
        for (bi, kstart, nk, tab, cc0, vslot, need) in binfo:
            kc0 = r + dil * kstart
            for h in range(2):
                S.op('pe', lambda t, h=h, kc0=kc0, nk=nk, bi=bi: t.matmul(
                    ps[0:nk, sb0 + h, bi * 128: bi * 128 + nq],
                    lhsT=kbuf[h * 64:(h + 1) * 64, kb_i, kc0:kc0 + dil * (nk - 1) + 1:dil],
                    rhs=qbuf[h * 64:(h + 1) * 64, kb_i, qcol0:qcol0 + dil * (nq - 1) + 1:dil],
                    start=True, stop=True),
                     reads=[('kbuf', kb_i), ('qbuf', kb_i)], writes=[('pss', par)])
        for (bi, kstart, nk, tab, cc0, vslot, need) in binfo:
            sview = ps[0:nk, sb0:sb0 + 2, bi * 128:bi * 128 + nq]
            S.op('act', lambda a_, sview=sview, nk=nk, bi=bi: a_.activation(
                out=es_t[0:nk, par, bi, :, 0:nq], in_=sview, func=AF.Exp),
                 reads=[('pss', par)], writes=[('es', par, bi)])
            if tab == 0:
                ev = E0[0:nk, gh0 * 256:(gh0 + 2) * 256].rearrange("p (a b) -> p a b", b=256)[:, :, cc0:cc0 + nq]
            else:
                ev = E1[0:nk, gh0 * 128:(gh0 + 2) * 128].rearrange("p (a b) -> p a b", b=128)[:, :, 0:nq]
            S.op('dve', lambda v, ev=ev, nk=nk, bi=bi: v.tensor_tensor(
                out=pb_t[0:nk, par, bi, :, 0:nq], in0=es_t[0:nk, par, bi, :, 0:nq], in1=ev, op=ALU.mult),
                 reads=[('es', par, bi), 'E0', 'E1'], writes=[('pb', par, bi)])

    def _b0_stage2(self, st, binfo, par, nq, g, qcol0, dil):
        S, ps = self.S, self.ps
        numbuf, denbuf = st['numbuf'], st['denbuf']
        vblk, pb_t = st['vblk'], st['pb']
        nb = len(binfo)
        for (bi, kstart, nk, tab, cc0, vslot, need) in binfo:
            S.op('pe', lambda t, vslot=vslot, nk=nk, bi=bi: t.matmul(
                ps[:, 2 + par, 0:256].rearrange("p (a b) -> p a b", b=128)[:, :, 0:nq],
                lhsT=vblk[0:nk, vslot, 0, :], rhs=pb_t[0:nk, par, bi, :, 0:nq],
                start=(bi == 0), stop=(bi == nb - 1)),
                 reads=[('vblk', vslot), ('pb', par, bi)], writes=[('psn', par)])
        for (bi, kstart, nk, tab, cc0, vslot, need) in binfo:
            S.op('pe', lambda t, nk=nk, bi=bi, vslot=vslot: t.matmul(
                ps[:, 4 + par, 0:256].rearrange("p (a b) -> p a b", b=128)[:, :, 0:nq],
                lhsT=vblk[0:nk, vslot, 1, :], rhs=pb_t[0:nk, par, bi, :, 0:nq],
                start=(bi == 0), stop=(bi == nb - 1)),
                 reads=[('vblk', vslot), ('pb', par, bi)], writes=[('psd', par)])
        qsl = slice(qcol0, qcol0 + dil * (nq - 1) + 1, dil)
        for h in range(2):
            S.op('act', lambda a_, h=h: a_.activation(
                out=numbuf[h * 64:(h + 1) * 64, g, qsl], in_=ps[h * 64:(h + 1) * 64, 2 + par, h * 128:h * 128 + nq],
                func=AF.Copy),
                 reads=[('psn', par)], writes=[('num', g)])
            if g == 0:
                S.op('dve', lambda v, h=h: v.tensor_copy(
                    out=denbuf[h * 64:(h + 1) * 64, qsl], in_=ps[h * 64:(h + 1) * 64, 4 + par, h * 128:h * 128 + nq]),
                     reads=[('psd', par)], writes=['den'])
            else:
                S.op('dve', lambda v, h=h: v.tensor_tensor(
                    out=denbuf[h * 64:(h + 1) * 64, qsl], in0=ps[h * 64:(h + 1) * 64, 4 + par, h * 128:h * 128 + nq],
                    in1=denbuf[h * 64:(h + 1) * 64, qsl], op=ALU.add),
                     reads=[('psd', par), 'den'], writes=['den'])

    def phaseB1(self, tiles, st):
        S, ps = self.S, self.ps
        xT, hT = self.xT, self.hT
        oT, vmask, gT = st['oT'], st['vmask'], st['gT']
        xk = [('xT', c) for c in range(KC)]
        hk = [('hT', c) for c in range(KC)]
        ok = [('oT', c) for c in range(9)]
        self.fm_io(self.XS1, 8, xT, tiles[0], False, xk)
        self.fm_io(self.OT, 9, oT, tiles[0], False, ok)
        for ti, pieces in enumerate(tiles):
            T = sum(n for _, n, _ in pieces)
            for tok0, n, col0 in pieces:
                d = dap(self.validrow_d, tok0, [(0, 128), (1, n)])
                S.op('pool', lambda e, d=d, col0=col0, n=n: e.dma_start(out=vmask[:, col0:col0 + n], in_=d),
                     writes=['vmask'], dma=True)
            for dc in range(8):
                slot = self.wload('wo', dc)
                wv = self.wview(slot, [9, 128])
                bank = self.outbank()
                for c in range(9):
                    S.op('pe', lambda t, c=c, bank=bank, wv=wv, T=T: t.matmul(
                        ps[:, bank, :T], lhsT=wv((c,), (0, 128)), rhs=oT[:, c, :T], start=(c == 0), stop=(c == 8)),
                         reads=[('w', slot), ('oT', c)], writes=[('ps', bank)])
                if dc > 0:
                    self.pre_mm(dc - 1, T)
                S.op('dve', lambda v, dc=dc, bank=bank, T=T: v.tensor_tensor(out=xT[:, dc, :T], in0=ps[:, bank, :T], in1=xT[:, dc, :T], op=ALU.add),
                     reads=[('ps', bank), ('xT', dc)], writes=[('xT', dc)])
                self.pre_sq(dc, T)
            self.pre_mm(7, T)
            if ti + 1 < len(tiles):
                self.fm_io(self.OT, 9, oT, tiles[ti + 1], False, ok)
            self.ffn(T, 3)
            self.ffn(T, 4)
            self.fm_io(self.XS2, 8, xT, pieces, True, xk)
            self.rmsnorm(T, self.xT_src(T), xk, VOFF[('nmx', 2)], lambda c, T=T: hT[:, c, :T], hk)
            if ti + 1 < len(tiles):
                self.fm_io(self.XS1, 8, xT, tiles[ti + 1], False, xk)
            cb1 = VOFF['cb1']
            for fb in range(4):
                slot = self.wload('pw1', fb)
                wv = self.wview(slot, [2, 8, 256])
                pbanks = [(self.mmbank(), self.mmbank()) for _ in range(2)]
                if fb == 0:
                    for kc in range(KC):
                        for jj in range(2):
                            for ag in range(2):
                                if (jj, ag) == (1, 1):
                                    continue
                                bank = pbanks[jj][ag]
                                S.op('pe', lambda t, ag=ag, bank=bank, kc=kc, jj=jj, wv=wv, T=T: t.matmul(
                                    ps[:, bank, :T], lhsT=wv((ag, kc), (jj * 128, 128)), rhs=hT[:, kc, :T],
                                    start=(kc == 0), stop=(kc == KC - 1)),
                                     reads=[('w', slot), ('hT', kc)], writes=[('ps', bank)])
                for jj in range(2):
                    c = fb * 2 + jj
                    ba, bg = pbanks[jj]
                    if fb != 0 or jj == 1:
                        for ag, bank in ((0, ba), (1, bg)):
                            if fb == 0 and ag == 0:
                                continue
                            for kc in range(KC):
                                S.op('pe', lambda t, ag=ag, bank=bank, kc=kc, jj=jj, wv=wv, T=T: t.matmul(
                                    ps[:, bank, :T], lhsT=wv((ag, kc), (jj * 128, 128)), rhs=hT[:, kc, :T],
                                    start=(kc == 0), stop=(kc == KC - 1)),
                                     reads=[('w', slot), ('hT', kc)], writes=[('ps', bank)])
                    ts = self.tmpslot()
                    S.op('act', lambda a, bg=bg, ts=ts, c=c, T=T: a.activation(out=self.tmpf[:, ts, :T], in_=ps[:, bg, :T], func=AF.Sigmoid,
                                                                                 bias=self.vecs[:, cb1 + 8 + c:cb1 + 9 + c]),
                         reads=[('ps', bg), 'vecs'], writes=[('tmpf', ts)])
                    S.op('dve', lambda v, ba=ba, ts=ts, c=c, T=T: v.scalar_tensor_tensor(
                        out=gT[:, c, :T], in0=ps[:, ba, :T], scalar=self.vecs[:, cb1 + c:cb1 + c + 1], in1=self.tmpf[:, ts, :T],
                        op0=ALU.add, op1=ALU.mult),
                         reads=[('ps', ba), ('tmpf', ts), 'vecs'], writes=[('gT', c)])
                    S.op('dve', lambda v, c=c, T=T: v.tensor_tensor(out=gT[:, c, :T], in0=gT[:, c, :T], in1=vmask[:, :T], op=ALU.mult),
                         reads=[('gT', c), 'vmask'], writes=[('gT', c)])
            self.fm_io(self.GT, 8, gT, pieces, True, [('gT', c) for c in range(8)])

    def phaseC(self, tiles, st):
        S, ps = self.S, self.ps
        xT, hT = self.xT, self.hT
        gpad, cacc = st['gpad'], st['cacc']
        xk = [('xT', c) for c in range(KC)]
        hk = [('hT', c) for c in range(KC)]
        gk = [('gpad', c) for c in range(KC)]
        ck = [('cacc', c) for c in range(KC)]
        cdw, cbdw, cgn, cb2 = VOFF['cdw'], VOFF['cbdw'], VOFF['cgn'], VOFF['cb2']
        T = TT
        def load_gpad(tile):
            tok0, seg_lo, seg_hi, out0 = tile
            lo = max(tok0 - 15, seg_lo)
            hi = min(tok0 + T + 15, seg_hi)
            c0 = lo - (tok0 - 15)
            n = hi - lo
            if c0 > 0:
                S.op('dve', lambda v, c0=c0: v.memset(gpad[:, :, 0:c0], 0.0), writes=gk)
            if c0 + n < T + 30:
                S.op('dve', lambda v, c0=c0, n=n: v.memset(gpad[:, :, c0 + n:T + 30], 0.0), writes=gk)
            self.fm_io(self.GT, 8, gpad, [(lo, n, c0)], False, gk)

        load_gpad(tiles[0])
        for ti, (tok0, seg_lo, seg_hi, out0) in enumerate(tiles):
            self.fm_io(self.XS2, 8, xT, [(tok0, T, 0)], False, xk)
            for c in range(KC):
                slot = self.wload('dwd', c)
                wv = self.wview(slot, [31, 128])
                bank = self.outbank()
                for j in range(31):
                    S.op('pe', lambda t, c=c, j=j, bank=bank, wv=wv: t.matmul(
                        ps[:, bank, :], lhsT=wv((j,), (0, 128)), rhs=gpad[:, c, j:j + T], start=(j == 0), stop=(j == 30)),
                         reads=[('w', slot), ('gpad', c)], writes=[('ps', bank)])
                S.op('act', lambda a_, c=c, bank=bank: a_.activation(out=cacc[:, c, :], in_=ps[:, bank, :], func=AF.Identity,
                                                                       bias=self.vecs[:, cbdw + c:cbdw + c + 1]),
                     reads=[('ps', bank), 'vecs'], writes=[('cacc', c)])
            if ti + 1 < len(tiles):
                load_gpad(tiles[ti + 1])
            self.rmsnorm(T, lambda c: cacc[:, c, :], ck, cgn, lambda c: cacc[:, c, :], ck)
            for c in range(KC):
                S.op('act', lambda a, c=c: a.activation(out=hT[:, c, :], in_=cacc[:, c, :], func=AF.Silu),
                     reads=[('cacc', c)], writes=[('hT', c)])
            for dc in range(8):
                slot = self.wload('pw2', dc)
                wv = self.wview(slot, [8, 128])
                bank = self.outbank()
                for kc in range(KC):
                    S.op('pe', lambda t, kc=kc, bank=bank, wv=wv: t.matmul(
                        ps[:, bank, :], lhsT=wv((kc,), (0, 128)), rhs=hT[:, kc, :], start=(kc == 0), stop=(kc == KC - 1)),
                         reads=[('w', slot), ('hT', kc)], writes=[('ps', bank)])
                S.op('dve', lambda v, dc=dc, bank=bank: v.scalar_tensor_tensor(
                    out=xT[:, dc, :], in0=ps[:, bank, :], scalar=self.vecs[:, cb2 + dc:cb2 + dc + 1], in1=xT[:, dc, :],
                    op0=ALU.add, op1=ALU.add),
                     reads=[('ps', bank), ('xT', dc), 'vecs'], writes=[('xT', dc)])
            self.ffn(T, 5)
            self.ffn(T, 6)
            self.gmlp(T, 1, 3)
            self.ffn(T, 7)
            self.rmsnorm(T, self.xT_src(T), xk, VOFF['nfin'], lambda c: cacc[:, c, :], ck)
            for sb in range(T // 128):
                xb = sb % 2
                for half in range(2):
                    bank = 6 + half
                    for cc in range(4):
                        c = half * 4 + cc
                        S.op('pe', lambda t, bank=bank, cc=cc, c=c, sb=sb: t.transpose(
                            ps[:, bank, cc * 128:(cc + 1) * 128], cacc[:, c, sb * 128:(sb + 1) * 128], self.ident[:]),
                             reads=[('cacc', c), 'ident'], writes=[('ps', bank)])
                    S.op('act', lambda a, bank=bank, half=half, xb=xb: a.activation(
                        out=self.xin[:, xb, half * 512:(half + 1) * 512], in_=ps[:, bank, :], func=AF.Copy),
                         reads=[('ps', bank)], writes=[('xin', xb)])
                dst = dap(self.y, (out0 + sb * 128) * D, [(D, 128), (1, D)])
                S.op('pool', lambda e, dst=dst, xb=xb: e.dma_start(out=dst, in_=self.xin[:, xb, :]), reads=[('xin', xb)], dma=True)

    def build(self):
        nc, es = self.nc, self.es
        with es:
            self.declare()
            self.S = Sched(nc, es)
            S = self.S
            self.load_consts()
            self.epsc = es.enter_context(nc.sbuf_tensor("epsc", [128, 1], F32))
            S.op('dve', lambda v: v.memset(self.epsc[:], EPS), writes=['epsc'])
            with nc.sbuf_tensor("wsin", [128, 2, 128], F32) as wsin:
                self.load_wst(wsin)
                self.build_bias_rows()
                self.phase0()
                self.conv_rest = ['wo', 'f1_3', 'f2_3', 'f1_4', 'f2_4', 'pw1', 'pw2', 'f1_5', 'f2_5', 'f1_6', 'f2_6',
                                  'ain_1', 'aout_1', 'f1_7', 'f2_7']
                if self.stop_after == 0:
                    self.emit_conv(n=100000)
                S.flush(skip_pool_prefix=True)
            if self.stop_after == 0:
                S.flush(final=True)
                return nc
            def alloc_common(e2, sfx):
                self.xT = e2.enter_context(nc.sbuf_tensor("xT" + sfx, [128, 8, TT], F32))
                self.hT = e2.enter_context(nc.sbuf_tensor("hT" + sfx, [128, 8, TT], BF16))
                self.rt = e2.enter_context(nc.sbuf_tensor("rt" + sfx, [128, 2, TT], F32))
                self.big = e2.enter_context(nc.sbuf_tensor("big" + sfx, [128, 48 * TT], BF16))
                self.tmpf = e2.enter_context(nc.sbuf_tensor("tmpf" + sfx, [128, 2, TT], F32))
                self.wsts = e2.enter_context(nc.sbuf_tensor("wsts" + sfx, [128, 4, 1024], BF16))
                self.bl = e2.enter_context(nc.sbuf_tensor("bl" + sfx, [128, AH], BF16))
                self.br = e2.enter_context(nc.sbuf_tensor("br" + sfx, [128, 4096], BF16))
                S.op('dve', lambda v: v.memset(self.bl[:], 0.0), writes=['bl'])
                S.op('dve', lambda v: v.memset(self.br[:], 0.0), writes=['br'])
                self.vst = e2.enter_context(nc.sbuf_tensor("vstat" + sfx, [128, 40], F32))
                self.junk = e2.enter_context(nc.sbuf_tensor("junk" + sfx, [128, 512], BF16))
                self.wring = e2.enter_context(nc.sbuf_tensor("wring" + sfx, [128, NSLOT, SLOT_ELEMS], BF16))
                self.xin = e2.enter_context(nc.sbuf_tensor("xin" + sfx, [128, 4, D], F32))
            tilesA = [(t, TT) for t in range(0, NP_, TT)]
            t = NP_
            while t < NL:
                T = min(TT, NL - t)
                tilesA.append((t, T))
                t += T
            if self.dbg and 'tilesA' in self.dbg:
                tilesA = self.dbg['tilesA']
            with ExitStack() as e2:
                alloc_common(e2, '_a')
                qk = e2.enter_context(nc.sbuf_tensor("qk", [128, 18, TT], BF16))
                vstg = e2.enter_context(nc.sbuf_tensor("vstg", [128, 4, VW], BF16))
                self.qk, self.vstg = qk, vstg
                first = ['f1_0', 'f2_0', 'ain_0', 'aout_0', 'f1_1', 'f2_1', 'f1_2', 'f2_2', 'qkv']
                self.emit_conv(streams=first)
                self.phaseA(tilesA)
                S.flush()
            if self.stop_after == 'A':
                S.flush(final=True)
                return nc
            NQM = OWN + 2 * QH
            with nc.sbuf_tensor("kbuf", [128, 2, NS], BF16) as kbuf, \
                    nc.sbuf_tensor("qbuf", [128, 2, NQM], BF16) as qbuf, \
                    nc.sbuf_tensor("E0", [128, 18 * 256], F32) as E0, \
                    nc.sbuf_tensor("E1", [128, 18 * 128], F32) as E1, \
                    nc.sbuf_tensor("numbuf", [128, 3, NQM], F32) as numbuf, \
                    nc.sbuf_tensor("denbuf", [128, NQM], F32) as denbuf, \
                    nc.sbuf_tensor("ostage", [128, NQM], BF16) as ostage, \
                    nc.sbuf_tensor("vblk", [128, 8, 2, 128], BF16) as vblk, \
                    nc.sbuf_tensor("es_t", [128, 2, 2, 2, 128], F32) as es_t, \
                    nc.sbuf_tensor("pb_t", [128, 2, 2, 2, 128], BF16) as pb_t, \
                    nc.sbuf_tensor("dg", [128, 2, 3968], BF16) as dg:
                cdw = VOFF['cdw']
                for c in range(8):
                    for j in range(31):
                        S.op('dve', lambda v, c=c, j=j: v.tensor_scalar(out=dg[:, c % 2, j * 128:(j + 1) * 128], in0=self.ident[:],
                                                                         scalar1=self.vecs[:, cdw + j * 8 + c:cdw + j * 8 + c + 1],
                                                                         scalar2=None, op0=ALU.mult),
                             reads=['ident', 'vecs'], writes=[('dg', c % 2)])
                    dd = dap(self.wbf['dwd'], c * 128 * 3968, [(3968, 128), (1, 3968)])
                    S.op('sp', lambda q, dd=dd, c=c: q.dma_start(out=dd, in_=dg[:, c % 2, :]), reads=[('dg', c % 2)], writes=[('dwd', c, 0)], dma=True)
                self.phaseB0(dict(kbuf=kbuf, qbuf=qbuf, E0=E0, E1=E1, numbuf=numbuf, denbuf=denbuf, ostage=ostage,
                                  vblk=vblk, es=es_t, pb=pb_t))
                S.flush()
            if self.stop_after == 'B0':
                S.flush(final=True)
                return nc
            tilesB = [[(t, TT, 0)] for t in range(0, NP_, TT)]
            s0 = NP_ + HALO
            tilesB += [[(s0 + t, TT, 0)] for t in range(0, OWN, TT)]
            tilesB.append([(s0 - QH, QH, 0), (s0 + OWN, QH, QH)])
            with ExitStack() as e3:
                alloc_common(e3, '_b')
                with nc.sbuf_tensor("oT", [128, 9, TT], BF16) as oT, nc.sbuf_tensor("vmask", [128, TT], F32) as vmask, \
                        nc.sbuf_tensor("gT", [128, 8, TT], BF16) as gT:
                    self.phaseB1(tilesB, dict(oT=oT, vmask=vmask, gT=gT))
                    S.flush()
                if self.stop_after == 'B1':
                    S.flush(final=True)
                    return nc
                tilesC = [(t, 0, NP_, t) for t in range(0, NP_, TT)]
                tilesC += [(s0 + t, NP_, NL, NP_ + t) for t in range(0, OWN, TT)]
                with nc.sbuf_tensor("gpad", [128, 8, TT + 30], BF16) as gpad, nc.sbuf_tensor("cacc", [128, 8, TT], F32) as cacc:
                    self.phaseC(tilesC, dict(gpad=gpad, cacc=cacc))
                    S.flush(final=True)
        return nc


def _t5_bucket(rel):
    half = 16
    max_exact = 8
    ret = np.where(rel > 0, half, 0)
    n = np.abs(rel)
    nf = np.maximum(n, 1).astype(np.float32)
    large = max_exact + (np.log(nf / np.float32(max_exact)) / np.float32(math.log(1024 / max_exact))
                         * np.float32(half - max_exact)).astype(np.int32)
    large = np.minimum(large, half - 1)
    return ret + np.where(n < max_exact, n, large)


def _bias_tables(rel_bias):
    kk = np.arange(128)[:, None]
    cc = np.arange(256)[None, :]
    rel = kk - cc + 64
    ok = np.abs(rel) <= 64
    b0 = np.full((128, 18, 256), -30000.0, np.float32)
    for g, dil in enumerate(DILS):
        bk = _t5_bucket((rel * dil).astype(np.int32))
        for h in range(6):
            gh = g * 6 + h
            b0[:, gh, :] = np.where(ok, rel_bias[bk, gh], np.float32(-30000.0))
    b1 = np.full((128, 18, 128), -30000.0, np.float32)
    b1[0:64] = b0[64:128, :, 128:256]
    return np.ascontiguousarray(b0.reshape(128, 18 * 256)), np.ascontiguousarray(b1.reshape(128, 18 * 128))


def _cols(v, n):
    return np.ascontiguousarray(np.asarray(v, np.float32).reshape(n, 128).T)


_PROG = {}


def kernel(**inp):
    x_prompt = np.asarray(inp['x_prompt'], np.float32)
    x_sample = np.asarray(inp['x_sample'], np.float32)
    vecs = np.zeros((128, NV), np.float32)
    for l in range(4):
        vecs[:, VOFF[('nf1', l)]:VOFF[('nf1', l)] + 8] = _cols(inp['norm_ffn1'][l], 8)
        vecs[:, VOFF[('nmx', l)]:VOFF[('nmx', l)] + 8] = _cols(inp['norm_mix'][l], 8)
        vecs[:, VOFF[('nf2', l)]:VOFF[('nf2', l)] + 8] = _cols(inp['norm_ffn2'][l], 8)
    vecs[:, VOFF['nfin']:VOFF['nfin'] + 8] = _cols(inp['norm_final'], 8)
    for j in range(2):
        vecs[:, VOFF[('agv', j)]:VOFF[('agv', j)] + 24] = _cols(inp['a_g_v'][j], 24)
    vecs[:, VOFF['cb1']:VOFF['cb1'] + 16] = _cols(inp['c_b_pw1'][0], 16)
    wdw = np.asarray(inp['c_w_dw'][0], np.float32)
    vecs[:, VOFF['cdw']:VOFF['cdw'] + 248] = wdw.reshape(31, 8, 128).transpose(2, 0, 1).reshape(128, 248)
    vecs[:, VOFF['cbdw']:VOFF['cbdw'] + 8] = _cols(inp['c_b_dw'][0], 8)
    vecs[:, VOFF['cgn']:VOFF['cgn'] + 8] = _cols(inp['c_g_norm'][0], 8)
    vecs[:, VOFF['cb2']:VOFF['cb2'] + 8] = _cols(inp['c_b_pw2'][0], 8)
    b0, b1 = _bias_tables(np.asarray(inp['rel_bias'], np.float32))
    ident = np.eye(128, dtype=np.float32)
    m4 = np.array([[1, 0, 1, 0], [1, 0, 0, 1], [0, 1, 1, 0], [0, 1, 0, 1]], np.float32)
    shared = {k: np.ascontiguousarray(np.asarray(inp[k], np.float32)) for k in (
        "ffn1_w_gate", "ffn1_w_up", "ffn1_w_down", "ffn2_w_gate", "ffn2_w_up", "ffn2_w_down",
        "a_w_in", "a_w_out", "a_w_spatial", "a_b_spatial", "a_g_v", "b_w_qkv", "b_w_out", "c_w_pw1", "c_w_pw2")}
    in_maps = []
    for core in range(NCORES):
        s, q = core // 4, core % 4
        g0 = q * OWN - HALO
        xl = np.zeros((NL, D), np.float32)
        xl[0:NP_] = x_prompt[core]
        valid = np.zeros((NL,), np.float32)
        valid[0:NP_] = 1.0
        a, b = max(g0, 0), min(g0 + NS, x_sample.shape[1])
        xl[NP_ + (a - g0):NP_ + (b - g0)] = x_sample[s, a:b]
        valid[NP_ + (a - g0):NP_ + (b - g0)] = 1.0
        m = dict(shared)
        m.update(xl=xl, vecs=vecs, validcol=np.ascontiguousarray(valid.reshape(NL // 128, 128).T),
                 validrow=valid.reshape(1, NL).copy(), ident=ident, braw0=b0, braw1=b1, m4=m4)
        in_maps.append(m)
    if 'nc' not in _PROG:
        _PROG['nc'] = Builder().build()
    res = run_bass_kernel_spmd(_PROG['nc'], in_maps, core_ids=list(range(NCORES)))
    y_prompt = np.zeros_like(x_prompt)
    y_sample = np.zeros_like(x_sample)
    for core in range(NCORES):
        s, q = core // 4, core % 4
        y = np.asarray(res.results[core]["y"])
        y_prompt[core] = y[0:NP_]
        y_sample[s, q * OWN:(q + 1) * OWN] = y[NP_:NP_ + OWN]
    return (y_prompt, y_sample)
```
